# Optimizing a Trainium2 kernel written in Bass

```python
import jax, jax.numpy as jnp
from jax import lax
import numpy as np

D_MODEL = 2048
BATCH = 16
SEQ = 2048
DEPTH = 1

CHUNK = 64
LEFT_CHUNKS = 8
BAND = (LEFT_CHUNKS + 1) * CHUNK

D_MIX = D_MODEL
D_ATTN = D_MIX // 2
ATTN_HEADS = 16
ATTN_HEAD_DIM = D_ATTN // ATTN_HEADS
REL_CLIP = 128
D_POOL = D_MIX - D_ATTN
POOL_WINDOWS = (2, 4, 8, 16)
N_POOL_GROUPS = len(POOL_WINDOWS)
POOL_GROUP_DIM = D_POOL // N_POOL_GROUPS
D_IN = 3 * D_ATTN + D_POOL

N_MEM = 256
CROSS_HEADS = 4
CROSS_HEAD_DIM = 128
D_CROSS = CROSS_HEADS * CROSS_HEAD_DIM

D_FF = ((8 * D_MODEL // 3) + 255) // 256 * 256
FFN_RES_WEIGHT = 0.5
EPS = 1e-6
NEG_INF = -1e30

kernel_name = "hybrid_chunk_attn_pool_macaron"


def rmsnorm(x, g):
    xf = x.astype(jnp.float32)
    y = xf * lax.rsqrt(jnp.mean(xf * xf, axis=-1, keepdims=True) + EPS)
    return (y * g.astype(jnp.float32)).astype(x.dtype)


def swiglu(x, w_gate, w_up, w_down):
    return (jax.nn.silu(x @ w_gate) * (x @ w_up)) @ w_down


def chunk_rel_attention(q, k, v, rel_table):
    B, S, H, Dh = q.shape
    nc = S // CHUNK
    pad = LEFT_CHUNKS * CHUNK
    kp = jnp.pad(k, ((0, 0), (pad, 0), (0, 0), (0, 0)))
    vp = jnp.pad(v, ((0, 0), (pad, 0), (0, 0), (0, 0)))
    q_band = pad + jnp.arange(CHUNK)
    rel = q_band[:, None] - jnp.arange(BAND)[None, :]
    rel_idx = jnp.clip(rel, -REL_CLIP, REL_CLIP) + REL_CLIP
    bias = rel_table[:, rel_idx].astype(jnp.float32)
    scale = Dh ** -0.5
    qc = q.reshape(B, nc, CHUNK, H, Dh).transpose(1, 0, 2, 3, 4)

    def one_chunk(args):
        c, qb = args
        start = c * CHUNK
        kb = lax.dynamic_slice_in_dim(kp, start, BAND, axis=1)
        vb = lax.dynamic_slice_in_dim(vp, start, BAND, axis=1)
        s = jnp.einsum('bqhd,bkhd->bhqk', qb, kb).astype(jnp.float32) * scale + bias[None]
        valid = (start + jnp.arange(BAND)) >= pad
        s = jnp.where(valid[None, None, None, :], s, NEG_INF)
        p = jax.nn.softmax(s, axis=-1).astype(vb.dtype)
        return jnp.einsum('bhqk,bkhd->bqhd', p, vb)

    out = lax.map(one_chunk, (jnp.arange(nc), qc))
    return out.transpose(1, 0, 2, 3, 4).reshape(B, S, H * Dh)


def multiscale_pool(u, w_pool, pool_scale):
    B, S, _ = u.shape
    uf = u.astype(jnp.float32)
    cs = jnp.concatenate([jnp.zeros((B, 1, D_POOL), jnp.float32), jnp.cumsum(uf, axis=1)], axis=1)
    t = jnp.arange(S)
    diffs = []
    for g, w in enumerate(POOL_WINDOWS):
        lo = jnp.maximum(t + 1 - w, 0)
        count = (t + 1 - lo).astype(jnp.float32)
        csg = cs[..., g * POOL_GROUP_DIM:(g + 1) * POOL_GROUP_DIM]
        mean = (csg[:, 1:] - csg[:, lo]) / count[None, :, None]
        diffs.append(mean - uf[..., g * POOL_GROUP_DIM:(g + 1) * POOL_GROUP_DIM])
    d = jnp.stack(diffs, axis=2).astype(u.dtype)
    y = jnp.einsum('bsgc,gcd->bsgd', d, w_pool).reshape(B, S, D_POOL)
    return y * pool_scale


def memory_cross_attention(h, mem, w_cq, w_ckv, w_co):
    B, S, _ = h.shape
    q = (h @ w_cq).reshape(B, S, CROSS_HEADS, CROSS_HEAD_DIM)
    kv = mem @ w_ckv
    k = kv[..., :D_CROSS].reshape(B, N_MEM, CROSS_HEADS, CROSS_HEAD_DIM)
    v = kv[..., D_CROSS:].reshape(B, N_MEM, CROSS_HEADS, CROSS_HEAD_DIM)
    s = jnp.einsum('bshd,bmhd->bhsm', q, k).astype(jnp.float32) * (CROSS_HEAD_DIM ** -0.5)
    p = jax.nn.softmax(s, axis=-1).astype(v.dtype)
    o = jnp.einsum('bhsm,bmhd->bshd', p, v).reshape(B, S, D_CROSS)
    return o @ w_co


def setup_inputs(seed: int = 0) -> dict:
    key = jax.random.key(seed)
    ks = jax.random.split(key, 24)
    f32 = jnp.float32

    def w(k, shape, fan_in):
        return jax.random.normal(k, shape, f32) * (fan_in ** -0.5)

    def gain(k, shape):
        return 1.0 + 0.05 * jax.random.normal(k, shape, f32)

    L = DEPTH
    return {
        "x": jax.random.normal(ks[0], (BATCH, SEQ, D_MODEL), f32),
        "mem": jax.random.normal(ks[1], (BATCH, N_MEM, D_MODEL), f32),
        "ffn1_norm": gain(ks[2], (L, D_MODEL)),
        "ffn1_w_gate": w(ks[3], (L, D_MODEL, D_FF), D_MODEL),
        "ffn1_w_up": w(ks[4], (L, D_MODEL, D_FF), D_MODEL),
        "ffn1_w_down": w(ks[5], (L, D_FF, D_MODEL), D_FF),
        "mix_norm": gain(ks[6], (L, D_MODEL)),
        "w_in": w(ks[7], (L, D_MODEL, D_IN), D_MODEL),
        "rel_bias": 0.5 * jax.random.normal(ks[8], (L, ATTN_HEADS, 2 * REL_CLIP + 1), f32),
        "w_pool": w(ks[9], (L, N_POOL_GROUPS, POOL_GROUP_DIM, POOL_GROUP_DIM), POOL_GROUP_DIM),
        "pool_scale": gain(ks[10], (L, D_POOL)),
        "w_out": w(ks[11], (L, D_MIX, D_MODEL), D_MIX),
        "cross_norm": gain(ks[12], (L, D_MODEL)),
        "mem_norm": gain(ks[13], (L, D_MODEL)),
        "w_cq": w(ks[14], (L, D_MODEL, D_CROSS), D_MODEL),
        "w_ckv": w(ks[15], (L, D_MODEL, 2 * D_CROSS), D_MODEL),
        "w_co": w(ks[16], (L, D_CROSS, D_MODEL), D_CROSS),
        "ffn2_norm": gain(ks[17], (L, D_MODEL)),
        "ffn2_w_gate": w(ks[18], (L, D_MODEL, D_FF), D_MODEL),
        "ffn2_w_up": w(ks[19], (L, D_MODEL, D_FF), D_MODEL),
        "ffn2_w_down": w(ks[20], (L, D_FF, D_MODEL), D_FF),
        "final_norm": gain(ks[21], (D_MODEL,)),
    }


def reference(x, mem, ffn1_norm, ffn1_w_gate, ffn1_w_up, ffn1_w_down, mix_norm, w_in,
              rel_bias, w_pool, pool_scale, w_out, cross_norm, mem_norm, w_cq, w_ckv, w_co,
              ffn2_norm, ffn2_w_gate, ffn2_w_up, ffn2_w_down, final_norm):
    B, S, _ = x.shape
    h = x
    for l in range(DEPTH):
        h = h + FFN_RES_WEIGHT * swiglu(rmsnorm(h, ffn1_norm[l]), ffn1_w_gate[l], ffn1_w_up[l], ffn1_w_down[l])
        z = rmsnorm(h, mix_norm[l]) @ w_in[l]
        q = z[..., 0 * D_ATTN:1 * D_ATTN].reshape(B, S, ATTN_HEADS, ATTN_HEAD_DIM)
        k = z[..., 1 * D_ATTN:2 * D_ATTN].reshape(B, S, ATTN_HEADS, ATTN_HEAD_DIM)
        v = z[..., 2 * D_ATTN:3 * D_ATTN].reshape(B, S, ATTN_HEADS, ATTN_HEAD_DIM)
        u = z[..., 3 * D_ATTN:]
        y_attn = chunk_rel_attention(q, k, v, rel_bias[l])
        y_pool = multiscale_pool(u, w_pool[l], pool_scale[l])
        h = h + jnp.concatenate([y_attn, y_pool], axis=-1) @ w_out[l]
        h = h + memory_cross_attention(rmsnorm(h, cross_norm[l]), rmsnorm(mem, mem_norm[l]),
                                       w_cq[l], w_ckv[l], w_co[l])
        h = h + FFN_RES_WEIGHT * swiglu(rmsnorm(h, ffn2_norm[l]), ffn2_w_gate[l], ffn2_w_up[l], ffn2_w_down[l])
    return rmsnorm(h, final_norm)
```

```python
import contextlib
import numpy as np
import concourse.bass as bass
import concourse.mybir as mybir
from concourse.bass_utils import run_bass_kernel_spmd

F32 = mybir.dt.float32
BF16 = mybir.dt.bfloat16
I32 = mybir.dt.int32
AF = mybir.ActivationFunctionType
ALU = mybir.AluOpType

NCORES = 8
D = 2048
DFF = 5632
T = 512
SEQ = 2048
TPS = SEQ // T
NSEQ = 2
NMEM = 256
KC = D // 128
EPS = 1e-6
NSLOT = 6
SLOTW = 4096
NPAGE = 34
import os
PRUNE = os.environ.get('K_PRUNE', '1') == '1'
DIRECT = os.environ.get('K_DIRECT', '0') == '1'


class Op:
    __slots__ = ("eng", "fn", "deps", "lane", "signal", "semval")

    def __init__(self, eng, fn, deps, lane):
        self.eng, self.fn, self.deps, self.lane = eng, fn, deps, lane
        self.signal = False
        self.semval = None


class Prog:
    ENGS = ("pe", "act", "dve", "pool", "sp")

    def __init__(self, dry=False):
        self.ops = []
        self.lastw = {}
        self.readers = {}
        self.dry = dry

    def add(self, eng, fn, reads=(), writes=(), lane=None):
        if self.dry:
            return
        i = len(self.ops)
        writes = list(writes)
        if lane is not None:
            writes.append(("lane", lane))
        deps = set()
        for r in reads:
            w = self.lastw.get(r)
            if w is not None:
                deps.add(w)
        for r in writes:
            w = self.lastw.get(r)
            if w is not None:
                deps.add(w)
            rs = self.readers.get(r)
            if rs:
                deps.update(rs)
        for r in reads:
            self.readers.setdefault(r, []).append(i)
        for r in writes:
            self.lastw[r] = i
            self.readers[r] = []
        deps.discard(i)
        self.ops.append(Op(eng, fn, deps, lane))

    def emit(self, nc, ctx):
        ops = self.ops
        for o in ops:
            if o.eng == "pe":
                o.deps = {d for d in o.deps if ops[d].eng != "pe"}
            if not PRUNE:
                continue
            best = {}
            keep = set()
            for d in o.deps:
                pe_ = ops[d]
                if pe_.lane is not None:
                    keep.add(d)
                elif best.get(pe_.eng, -1) < d:
                    best[pe_.eng] = d
            keep.update(best.values())
            o.deps = keep
        for o in ops:
            for d in o.deps:
                ops[d].signal = True
        lanes = sorted({o.lane for o in ops if o.lane is not None})
        sems = {}
        for e in ("pe", "act", "dve", "pool"):
            sems[e] = ctx.enter_context(nc.semaphore("s_" + e))
        for l in lanes:
            sems[("lane", l)] = ctx.enter_context(nc.semaphore("l_%s" % (l,)))
        cnt = {k: 0 for k in sems}
        for o in ops:
            if o.lane is not None:
                k = ("lane", o.lane)
                cnt[k] += 16
                o.semval = (k, cnt[k])
            elif o.signal:
                cnt[o.eng] += 1
                o.semval = (o.eng, cnt[o.eng])
        streams = {e: [o for o in ops if o.eng == e] for e in self.ENGS}
        block = ctx.enter_context(nc.Block())

        def run(stream, e):
            waited = {}
            for o in stream:
                need = {}
                for d in o.deps:
                    k, v = ops[d].semval
                    if need.get(k, 0) < v:
                        need[k] = v
                for k, v in need.items():
                    if waited.get(k, 0) < v:
                        e.wait_ge(sems[k], v)
                        waited[k] = v
                ins = o.fn(e)
                if o.semval is not None:
                    ins.then_inc(sems[o.semval[0]], 16 if o.lane is not None else 1)

        @block.tensor
        def _(e):
            run(streams["pe"], e)

        @block.scalar
        def _(e):
            run(streams["act"], e)

        @block.vector
        def _(e):
            run(streams["dve"], e)

        @block.gpsimd
        def _(e):
            run(streams["pool"], e)
            for l in lanes:
                if str(l).startswith("o"):
                    e.wait_ge(sems[("lane", l)], cnt[("lane", l)])

        @block.sync
        def _(e):
            run(streams["sp"], e)


def build_program(ntiles=NSEQ * TPS):
    nc = bass.Bass("TRN2", target_bir_lowering=False)
    ntok = ntiles * T
    dr = lambda name, shape, dt=F32, kind="ExternalInput": nc.dram_tensor(name, shape, dt, kind=kind).ap()
    x_d = dr("x", [NSEQ * SEQ, D])
    mem_d = dr("mem", [NSEQ * NMEM, D])
    out_d = dr("out", [NSEQ * SEQ, D], kind="ExternalOutput")
    cst_d = dr("cst", [128, 128])
    bm_d = dr("bmat", [128, 16 * 256])
    W = {}
    for f in ("ffn1", "ffn2"):
        W[f + "_g"] = dr(f + "_g", [D, DFF])
        W[f + "_u"] = dr(f + "_u", [D, DFF])
        W[f + "_d"] = dr(f + "_d", [DFF, D])
    W["w_in"] = dr("w_in", [D, 4096])
    W["w_pool"] = dr("w_pool", [1024, 256])
    W["w_out"] = dr("w_out", [D, D])
    W["w_cq"] = dr("w_cq", [D, 512])
    W["w_ckv"] = dr("w_ckv", [D, 1024])
    W["w_co"] = dr("w_co", [512, D])

    wblocks = {}
    for f in ("ffn1", "ffn2"):
        wblocks[f + "_g"] = [(0, 16, j * 256, 256) for j in range(22)]
        wblocks[f + "_u"] = [(0, 16, j * 256, 256) for j in range(22)]
        wblocks[f + "_d"] = [((hh * 22 + rg * 11) * 128, 11, cb * 256, 256)
                             for hh in range(2) for rg in range(2) for cb in range(8)]
    wblocks["w_in"] = [(0, 16, j * 256, 256) for j in range(16)]
    wblocks["w_pool"] = [(0, 8, 0, 256)]
    wblocks["w_out"] = [(0, 16, j * 256, 256) for j in range(8)]
    wblocks["w_cq"] = [(0, 16, j * 256, 256) for j in range(2)]
    wblocks["w_ckv"] = [(0, 16, j * 256, 256) for j in range(4)]
    wblocks["w_co"] = [(0, 4, j * 1024, 1024) for j in range(2)]
    scr = {}
    for name, bl in wblocks.items():
        scr[name] = nc.dram_tensor("scr_" + name, [len(bl), 128, bl[0][1] * bl[0][3]], BF16).ap()

    with contextlib.ExitStack() as ctx:
        sb = lambda name, shape, dt: ctx.enter_context(nc.sbuf_tensor(name, shape, dt))
        hT = sb("hT", [128, KC, T], F32)
        xnT = sb("xnT", [128, KC, T], BF16)
        arena = sb("arena", [128, NPAGE * 512], BF16)
        KT = sb("KT", [128, 8, 2, T], BF16)
        V = sb("V", [128, 8, 16, 65], BF16)
        E = sb("E", [128, 16, 256], BF16)
        xst = [sb("xst%d" % i, [128, D], F32) for i in range(2)]
        wsl = [sb("wsl%d" % i, [128, SLOTW], BF16) for i in range(NSLOT)]
        KcT = sb("KcT", [128, 4, NMEM], BF16)
        Vc = sb("Vc", [128, 2, 512], BF16)
        sq = [sb("sq%d" % i, [128, T], BF16) for i in range(3)]
        dG1 = sb("dG1", [128, 2, T], BF16)
        rstd = sb("rstd", [128, T], F32)
        sg = [sb("sg%d" % i, [128, T], F32) for i in range(2)]
        cst = sb("cst_sb", [128, 128], F32)
        ident = sb("ident", [128, 128], F32)
        ones = sb("ones", [128, 128], BF16)
        halo = sb("halo", [128, 8, 16], F32)
        invc = sb("invc", [128, 16], F32)
        invc_i = sb("invc_i", [128, 16], I32)
        rcp = sb("rcp", [128, 16], F32)
        mst = sb("mst", [128, 4], F32)
        nbf = sb("nbf", [128, 16], F32)
        rT = sb("rT", [128, 4], F32)
        wpool_sb = sb("wpool_sb", [128, 8, 256], BF16)
        psum = ctx.enter_context(nc.psum_tensor("psum", [128, 8, 512], F32))

        GC = {"ffn1": 0, "mix": 16, "cross": 32, "ffn2": 48, "final": 64, "mem": 80}
        PSC = 96
        BFAR = 104

        def pg(p0, npg):
            return [("ar", p) for p in range(p0, p0 + npg)]

        def ar_bf(p0, npages):
            return arena[:, p0 * 512:(p0 + npages) * 512]

        def ar_f32(p0, npages):
            return arena[:, p0 * 512:(p0 + npages) * 512].bitcast(F32)

        hid = ar_bf(0, 22).rearrange("p (c t) -> p c t", t=T)
        QT = ar_bf(0, 8).rearrange("p (c t) -> p c t", t=T)
        yT = ar_bf(8, 16).rearrange("p (c t) -> p c t", t=T)
        uG = ar_f32(24, 5)[:, 0:1056].rearrange("p (c t) -> p c t", t=528)
        tA = ar_f32(29, 5)[:, 0:1056].rearrange("p (c t) -> p c t", t=528)
        tB = ar_f32(8, 5)[:, 0:1056].rearrange("p (c t) -> p c t", t=528)
        dG = ar_bf(13, 2).rearrange("p (c t) -> p c t", t=T)
        PT = [arena[:, 24 * 512 + i * 640:24 * 512 + (i + 1) * 640] for i in range(4)]
        ytok = ar_f32(30, 4)
        QcT = ar_bf(0, 4).rearrange("p (c t) -> p c t", t=T)
        PTc = [ar_bf(4 + 2 * i, 2).rearrange("p (c t) -> p c t", t=T) for i in range(2)]
        rden = ar_f32(8, 2)
        ocT = ar_bf(10, 4).rearrange("p (c t) -> p c t", t=T)
        memst = [ar_f32(8 * i, 8) for i in range(2)]
        mnT = ar_bf(16, 8).rearrange("p (c t) -> p c t", t=NMEM)
        bmst = ar_f32(0, 16)

        hk = lambda c: [("hT", c, tb) for tb in range(4)]

        def emit_all(P, ws):
            bank_ctr = [0]

            hold = set()

            def nb():
                while True:
                    b = bank_ctr[0] % 7
                    bank_ctr[0] += 1
                    if b not in hold:
                        return b

            SB = 7
            pend = []
            stat_n = [0]

            def tick():
                keep = []
                for ent in pend:
                    if ent[0] >= 1:
                        ent[1]()
                    else:
                        ent[0] += 1
                        keep.append(ent)
                pend[:] = keep

            def flush():
                for ent in pend:
                    ent[1]()
                pend[:] = []

            def ensure_free(tag):
                last = -1
                for i, ent in enumerate(pend):
                    if ent[2] == tag:
                        last = i
                for ent in pend[:last + 1]:
                    ent[1]()
                pend[:] = pend[last + 1:]

            def stat_hook(dc):
                k = stat_n[0]
                stat_n[0] += 1
                sbuf = sq[k % 3]
                ensure_free(k % 3)
                P.add("act", lambda e, dc=dc, sbuf=sbuf: e.activation(out=sbuf[:], in_=hT[:, dc, :], func=AF.Square),
                      reads=hk(dc), writes=[("sq", k % 3)])

                def mm(k=k, sbuf=sbuf):
                    P.add("pe", lambda e: e.matmul(psum[:, SB, :], ones[:], sbuf[:], start=(k == 0), stop=(k == KC - 1)),
                          reads=[("sq", k % 3), "ones"], writes=[("ps", SB)])
                pend.append([0, mm, k % 3])

            evq = [0]

            P.add("pool", lambda e: e.dma_start(out=cst[:], in_=cst_d), writes=["cst"], lane="x0")
            P.add("pool", lambda e: e.dma_start(out=bmst, in_=bm_d), writes=pg(0, 16), lane="x1")
            P.add("pool", lambda e: e.memset(ident[:], 0.0), writes=["ident"])
            P.add("pool", lambda e: e.affine_select(out=ident[:], in_=ident[:], pattern=[[-1, 128]],
                                                    compare_op=ALU.not_equal, fill=1.0, base=0,
                                                    channel_multiplier=1),
                  reads=["ident"], writes=["ident"])
            P.add("pool", lambda e: e.memset(ones[:], 1.0), writes=["ones"])
            P.add("pool", lambda e: e.memset(V[:], 1.0), writes=[("V", i) for i in range(8)])
            P.add("pool", lambda e: e.iota(invc_i[:], [[1, 16]], base=1, channel_multiplier=0),
                  writes=["invc_i"])
            P.add("dve", lambda e: e.tensor_copy(invc[:], invc_i[:]), reads=["invc_i"], writes=["invc"])
            P.add("dve", lambda e: e.reciprocal(invc[:], invc[:]), reads=["invc"], writes=["invc"])
            P.add("act", lambda e: e.activation(out=nbf[:], in_=cst[:, BFAR:BFAR + 16], func=AF.Copy, scale=-1.0),
                  reads=["cst"], writes=["nbf"])
            for h in range(16):
                P.add("act", lambda e, h=h: e.activation(
                    out=E[:, h, :], in_=bmst[:, h * 256:(h + 1) * 256], func=AF.Exp, bias=nbf[:, h:h + 1]),
                    reads=pg(h, 1) + ["nbf"], writes=[("E", h // 4)])
            P.add("dve", lambda e: e.memset(E[64:128, :, 128:192], 0.0),
                  reads=[("E", i) for i in range(4)], writes=[("E", i) for i in range(4)])

            def evac_copy(out, in_, reads, writes, scale=None):
                evq[0] += 1
                if scale is not None:
                    P.add("act", lambda e: e.activation(out=out, in_=in_, func=AF.Copy, scale=scale),
                          reads=reads, writes=writes)
                elif evq[0] % 2:
                    P.add("act", lambda e: e.activation(out=out, in_=in_, func=AF.Copy),
                          reads=reads, writes=writes)
                else:
                    P.add("dve", lambda e: e.tensor_copy(out, in_), reads=reads, writes=writes)

            def norm_stats(gname):
                flush()
                assert stat_n[0] == KC, stat_n[0]
                stat_n[0] = 0
                P.add("act", lambda e: e.activation(out=rstd[:], in_=psum[:, SB, :], func=AF.Sqrt,
                                                    scale=1.0 / D, bias=cst[:, 127:128]),
                      reads=[("ps", SB), "cst"], writes=["rstd"])
                P.add("dve", lambda e: e.reciprocal(rstd[:], rstd[:]), reads=["rstd"], writes=["rstd"])

            def norm(gname):
                norm_stats(gname)
                g0 = GC[gname]
                for c in range(KC):
                    P.add("dve", lambda e, c=c: e.scalar_tensor_tensor(
                        out=xnT[:, c, :], in0=hT[:, c, :], scalar=cst[:, g0 + c:g0 + c + 1], in1=rstd[:],
                        op0=ALU.mult, op1=ALU.mult),
                        reads=hk(c) + ["rstd", "cst"], writes=[("xn", c)])

            XN = [("xn", c) for c in range(KC)]

            def ffn(f, nxt):
                ci = 0
                for hh in range(2):
                    for fbl in range(11):
                        fb = hh * 11 + fbl
                        gs, gv = ws.get(f + "_g", fb)
                        us, uv = ws.get(f + "_u", fb)
                        first = (hh == 0 and fbl == 0)
                        banks = [(nb(), nb()) for _ in range(2)]
                        if first:
                            for kc in range(KC):
                                for c2 in range(2):
                                    for (bk, wv, sl) in ((banks[c2][0], gv, gs), (banks[c2][1], uv, us)):
                                        P.add("pe", lambda e, bk=bk, wv=wv, kc=kc, c2=c2: e.matmul(
                                            psum[:, bk, :], wv[:, kc, c2 * 128:(c2 + 1) * 128], xnT[:, kc, :],
                                            start=(kc == 0), stop=(kc == KC - 1)),
                                            reads=[("ws", sl), ("xn", kc)], writes=[("ps", bk)])
                                if kc == 3:
                                    norm_stats(f)
                        for c2 in range(2):
                            lc = fbl * 2 + c2
                            bg, bu = banks[c2]
                            if not first:
                                for (bk, wv, sl) in ((bg, gv, gs), (bu, uv, us)):
                                    for kc in range(KC):
                                        P.add("pe", lambda e, bk=bk, wv=wv, kc=kc, c2=c2: e.matmul(
                                            psum[:, bk, :], wv[:, kc, c2 * 128:(c2 + 1) * 128], xnT[:, kc, :],
                                            start=(kc == 0), stop=(kc == KC - 1)),
                                            reads=[("ws", sl), ("xn", kc)], writes=[("ps", bk)])
                            tick()
                            s = sg[ci % 2]
                            sk = ("sg", ci % 2)
                            P.add("dve", lambda e, s=s, bg=bg: e.tensor_tensor(
                                out=s[:], in0=psum[:, bg, :], in1=rstd[:], op=ALU.mult),
                                reads=[("ps", bg), "rstd"], writes=[sk])
                            P.add("act", lambda e, s=s: e.activation(out=s[:], in_=s[:], func=AF.Silu),
                                  reads=[sk], writes=[sk])
                            P.add("dve", lambda e, s=s, bu=bu: e.tensor_tensor(
                                out=s[:], in0=s[:], in1=psum[:, bu, :], op=ALU.mult),
                                reads=[sk, ("ps", bu)], writes=[sk])
                            P.add("dve", lambda e, s=s, lc=lc: e.tensor_tensor(
                                out=hid[:, lc, :], in0=s[:], in1=rstd[:], op=ALU.mult),
                                reads=[sk, "rstd"], writes=pg(lc, 1))
                            ci += 1
                    for cb in range(8):
                        d0s, d0v = ws.get(f + "_d", (hh * 2 + 0) * 8 + cb)
                        d1s, d1v = ws.get(f + "_d", (hh * 2 + 1) * 8 + cb)
                        for d2 in range(2):
                            dc = cb * 2 + d2
                            b = nb()
                            for rg, (dv, ds) in enumerate(((d0v, d0s), (d1v, d1s))):
                                for fi in range(11):
                                    lc = rg * 11 + fi
                                    P.add("pe", lambda e, b=b, dv=dv, fi=fi, d2=d2, lc=lc, rg=rg: e.matmul(
                                        psum[:, b, :], dv[:, fi, d2 * 128:(d2 + 1) * 128], hid[:, lc, :],
                                        start=(rg == 0 and fi == 0), stop=(rg == 1 and fi == 10)),
                                        reads=[("ws", ds)] + pg(lc, 1), writes=[("ps", b)])
                            tick()
                            P.add("dve", lambda e, b=b, dc=dc: e.scalar_tensor_tensor(
                                out=hT[:, dc, :], in0=psum[:, b, :], scalar=0.5, in1=hT[:, dc, :],
                                op0=ALU.mult, op1=ALU.add),
                                reads=[("ps", b)] + hk(dc), writes=hk(dc))
                            if hh == 1:
                                stat_hook(dc)
                                if nxt == "mix":
                                    xn_plain(dc, "mix")
                                elif nxt == "final":
                                    hg_inplace(dc)

            def proj_fm(wname, nblk, rhs_of_kc, rhs_keys, nkc, consume, cols_per_blk=256):
                for bi in range(nblk):
                    s, wv = ws.get(wname, bi)
                    for c2 in range(cols_per_blk // 128):
                        b = nb()
                        for kc in range(nkc):
                            P.add("pe", lambda e, b=b, wv=wv, kc=kc, c2=c2: e.matmul(
                                psum[:, b, :], wv[:, kc, c2 * 128:(c2 + 1) * 128], rhs_of_kc(kc),
                                start=(kc == 0), stop=(kc == nkc - 1)),
                                reads=[("ws", s)] + rhs_keys(kc), writes=[("ps", b)])
                        tick()
                        consume(bi * (cols_per_blk // 128) + c2, b)

            def xn_plain(dc, gname):
                g0 = GC[gname]
                P.add("act", lambda e: e.activation(out=xnT[:, dc, :], in_=hT[:, dc, :], func=AF.Copy,
                                                    scale=cst[:, g0 + dc:g0 + dc + 1]),
                      reads=hk(dc) + ["cst"], writes=[("xn", dc)])

            def make_rT():
                b = nb()
                for tb in range(4):
                    P.add("pe", lambda e, b=b, tb=tb: e.transpose(
                        psum[:, b, tb * 128:(tb + 1) * 128], rstd[:, tb * 128:(tb + 1) * 128], ident[:]),
                        reads=["rstd", "ident"], writes=[("ps", b)])
                P.add("dve", lambda e, b=b: e.tensor_copy(
                    rT[:, :].unsqueeze(2), psum[:, b, :].rearrange("p (t c) -> p t c", c=128)[:, :, 0:1]),
                    reads=[("ps", b)], writes=["rT"])

            def hg_inplace(dc):
                g0 = GC["final"]
                P.add("act", lambda e: e.activation(out=hT[:, dc, :], in_=hT[:, dc, :], func=AF.Copy,
                                                    scale=cst[:, g0 + dc:g0 + dc + 1]),
                      reads=hk(dc) + ["cst"], writes=hk(dc))

            def resid_add(dc, b, xn_for=None):
                P.add("dve", lambda e: e.tensor_tensor(out=hT[:, dc, :], in0=psum[:, b, :], in1=hT[:, dc, :],
                                                       op=ALU.add),
                      reads=[("ps", b)] + hk(dc), writes=hk(dc))
                stat_hook(dc)
                if xn_for is not None:
                    xn_plain(dc, xn_for)

            def seq_loads(s):
                for mb in range(2):
                    ms = memst[mb]
                    r0 = s * NMEM + mb * 128
                    P.add("pool", lambda e, ms=ms, r0=r0: e.dma_start(out=ms, in_=mem_d[r0:r0 + 128, :]),
                          writes=pg(8 * mb, 8), lane="x%d" % mb)

            def seq_start(s):
                P.add("dve", lambda e: e.memset(halo[:], 0.0), writes=["halo"])
                for mb in range(2):
                    ms = memst[mb]
                    P.add("act", lambda e, ms=ms, mb=mb: e.activation(out=ar_bf(24, 4), in_=ms, func=AF.Square,
                                                                      accum_out=mst[:, mb:mb + 1]),
                          reads=pg(8 * mb, 8), writes=pg(24, 4) + [("mst", mb)])
                    P.add("act", lambda e, mb=mb: e.activation(out=mst[:, mb:mb + 1], in_=mst[:, mb:mb + 1], func=AF.Sqrt,
                                                               scale=1.0 / D, bias=cst[:, 127:128]),
                          reads=[("mst", mb), "cst"], writes=[("mst", mb)])
                    P.add("dve", lambda e, mb=mb: e.reciprocal(mst[:, mb:mb + 1], mst[:, mb:mb + 1]),
                          reads=[("mst", mb)], writes=[("mst", mb)])
                    P.add("dve", lambda e, ms=ms, mb=mb: e.tensor_scalar(out=ms, in0=ms, scalar1=mst[:, mb:mb + 1],
                                                                       scalar2=None, op0=ALU.mult),
                          reads=pg(8 * mb, 8) + [("mst", mb)], writes=pg(8 * mb, 8))
                    for c4 in range(4):
                        b = nb()
                        for ci in range(4):
                            c = c4 * 4 + ci
                            P.add("pe", lambda e, b=b, ci=ci, c=c, ms=ms: e.transpose(
                                psum[:, b, ci * 128:(ci + 1) * 128], ms[:, c * 128:(c + 1) * 128], ident[:]),
                                reads=pg(8 * mb, 8) + ["ident"], writes=[("ps", b)])
                        for ci in range(4):
                            c = c4 * 4 + ci
                            P.add("act", lambda e, b=b, ci=ci, c=c, mb=mb: e.activation(
                                out=mnT[:, c, mb * 128:(mb + 1) * 128], in_=psum[:, b, ci * 128:(ci + 1) * 128],
                                func=AF.Copy, scale=cst[:, GC["mem"] + c:GC["mem"] + c + 1]),
                                reads=[("ps", b), "cst"], writes=pg(16, 8))
                for bi in range(2):
                    sl, wv = ws.get("w_ckv", bi)
                    for c2 in range(2):
                        b = nb()
                        for kc in range(KC):
                            P.add("pe", lambda e, b=b, wv=wv, kc=kc, c2=c2: e.matmul(
                                psum[:, b, 0:NMEM], wv[:, kc, c2 * 128:(c2 + 1) * 128], mnT[:, kc, :],
                                start=(kc == 0), stop=(kc == KC - 1)),
                                reads=[("ws", sl)] + pg(16, 8), writes=[("ps", b)])
                        hh = bi * 2 + c2
                        evac_copy(KcT[:, hh, :], psum[:, b, 0:NMEM], [("ps", b)], ["KcT"])
                for bi in range(2, 4):
                    sl, wv = ws.get("w_ckv", bi)
                    for mb in range(2):
                        b = nb()
                        for kc in range(KC):
                            P.add("pe", lambda e, b=b, wv=wv, kc=kc, mb=mb: e.matmul(
                                psum[:, b, 0:256], mnT[:, kc, mb * 128:(mb + 1) * 128], wv[:, kc, :],
                                start=(kc == 0), stop=(kc == KC - 1)),
                                reads=[("ws", sl)] + pg(16, 8), writes=[("ps", b)])
                        evac_copy(Vc[:, mb, (bi - 2) * 256:(bi - 1) * 256], psum[:, b, 0:256], [("ps", b)], ["Vc"])

            def issue_x(ti, tb):
                xs = xst[tb % 2]
                r0 = ti * T + tb * 128
                P.add("pool", lambda e, xs=xs, r0=r0: e.dma_start(out=xs[:], in_=x_d[r0:r0 + 128, :]),
                      writes=[("xst", tb % 2)], lane="x%d" % (tb % 2))

            def xpose_x(ti, tb):
                xs = xst[tb % 2]
                for c4 in range(4):
                    b = nb()
                    for ci in range(4):
                        c = c4 * 4 + ci
                        P.add("pe", lambda e, b=b, ci=ci, c=c, xs=xs: e.transpose(
                            psum[:, b, ci * 128:(ci + 1) * 128], xs[:, c * 128:(c + 1) * 128], ident[:]),
                            reads=[("xst", tb % 2), "ident"], writes=[("ps", b)])
                    tick()
                    hkeys = [("hT", c4 * 4 + ci, tb) for ci in range(4)]
                    evac_copy(hT[:, c4 * 4:(c4 + 1) * 4, tb * 128:(tb + 1) * 128],
                              psum[:, b, :].rearrange("p (c t) -> p c t", t=128), [("ps", b)], hkeys)
                    k = c4
                    sbuf = sq[k % 3]
                    ensure_free(k % 3)
                    P.add("act", lambda e, c4=c4, sbuf=sbuf: e.activation(
                        out=sbuf[:].rearrange("p (c t) -> p c t", t=128),
                        in_=hT[:, c4 * 4:(c4 + 1) * 4, tb * 128:(tb + 1) * 128], func=AF.Square),
                        reads=hkeys, writes=[("sq", k % 3)])

                    def mm(c4=c4, sbuf=sbuf, k=k):
                        for ci in range(4):
                            P.add("pe", lambda e, ci=ci: e.matmul(
                                psum[:, SB, tb * 128:(tb + 1) * 128], ones[:], sbuf[:, ci * 128:(ci + 1) * 128],
                                start=(c4 == 0 and ci == 0), stop=(c4 == 3 and ci == 3)),
                                reads=[("sq", k % 3), "ones"], writes=[("ps", SB)])
                    pend.append([0, mm, k % 3])
                if tb == 3:
                    stat_n[0] = KC
                    for c in range(KC):
                        xn_plain(c, "ffn1")

            def mixer(ti):
                tt = ti % TPS
                cur, prv = tt % 2, (tt + 1) % 2
                def q_blocks():
                    proj_fm("w_in", 4, lambda kc: xnT[:, kc, :], lambda kc: [("xn", kc)], KC,
                            lambda c, b: P.add("dve", lambda e: e.tensor_tensor(
                                out=QT[:, c, :], in0=psum[:, b, :], in1=rstd[:], op=ALU.mult),
                                reads=[("ps", b), "rstd"], writes=pg(c, 1)))

                def k_blocks():
                    for bi in range(4, 8):
                        sl, wv = ws.get("w_in", bi)
                        for c2 in range(2):
                            c = (bi - 4) * 2 + c2
                            b = nb()
                            for kc in range(KC):
                                P.add("pe", lambda e, b=b, wv=wv, kc=kc, c2=c2: e.matmul(
                                    psum[:, b, :], wv[:, kc, c2 * 128:(c2 + 1) * 128], xnT[:, kc, :],
                                    start=(kc == 0), stop=(kc == KC - 1)),
                                    reads=[("ws", sl), ("xn", kc)], writes=[("ps", b)])
                            tick()
                            P.add("dve", lambda e, b=b, c=c: e.tensor_tensor(
                                out=KT[:, c, cur, :], in0=psum[:, b, :], in1=rstd[:], op=ALU.mult),
                                reads=[("ps", b), "rstd"], writes=[("KT", c, cur)])

                def v_blocks(lo, hi):
                    for bi in range(lo, hi):
                        sl, wv = ws.get("w_in", bi)
                        vb = bi - 8
                        for tb in range(4):
                            b = nb()
                            for kc in range(KC):
                                P.add("pe", lambda e, b=b, wv=wv, kc=kc, tb=tb: e.matmul(
                                    psum[:, b, 0:256], xnT[:, kc, tb * 128:(tb + 1) * 128], wv[:, kc, :],
                                    start=(kc == 0), stop=(kc == KC - 1)),
                                    reads=[("ws", sl), ("xn", kc)], writes=[("ps", b)])
                            tick()
                            evac_copy(V[:, cur * 4 + tb, vb * 4:(vb + 1) * 4, 0:64],
                                      psum[:, b, 0:256].rearrange("p (h d) -> p h d", d=64),
                                      [("ps", b), "rT"], [("V", cur * 4 + tb)], scale=rT[:, tb:tb + 1])

                pwv = wpool_sb
                UK = pg(24, 5)

                def dbuf(g):
                    return (dG, pg(13, 2)) if g % 2 == 0 else (dG1, [("dg1",)])

                def u_block(g):
                    sl, wv = ws.get("w_in", 12 + g)
                    dg, DK = dbuf(g)
                    ub = [nb(), nb()]
                    if g == 0:
                        for kc in range(KC):
                            for c2 in range(2):
                                P.add("pe", lambda e, b=ub[c2], wv=wv, kc=kc, c2=c2: e.matmul(
                                    psum[:, b, :], wv[:, kc, c2 * 128:(c2 + 1) * 128], xnT[:, kc, :],
                                    start=(kc == 0), stop=(kc == KC - 1)),
                                    reads=[("ws", sl), ("xn", kc)], writes=[("ps", ub[c2])])
                            if kc == 7:
                                norm_stats("mix")
                    for c2 in range(2):
                        b = ub[c2]
                        if g != 0:
                            for kc in range(KC):
                                P.add("pe", lambda e, b=b, wv=wv, kc=kc, c2=c2: e.matmul(
                                    psum[:, b, :], wv[:, kc, c2 * 128:(c2 + 1) * 128], xnT[:, kc, :],
                                    start=(kc == 0), stop=(kc == KC - 1)),
                                    reads=[("ws", sl), ("xn", kc)], writes=[("ps", b)])
                        tick()
                        P.add("dve", lambda e, b=b, c2=c2: e.tensor_tensor(
                            out=uG[:, c2, 16:528], in0=psum[:, b, :], in1=rstd[:], op=ALU.mult),
                            reads=[("ps", b), "rstd"], writes=UK)
                    P.add("dve", lambda e, g=g: e.tensor_copy(uG[:, :, 0:16], halo[:, 2 * g:2 * g + 2, :]),
                          reads=["halo"], writes=UK)
                    P.add("dve", lambda e, g=g: e.tensor_copy(halo[:, 2 * g:2 * g + 2, :], uG[:, :, 512:528]),
                          reads=UK, writes=["halo"])
                    src, skeys = uG, UK
                    bufs = [(tA, pg(29, 5)), (tB, pg(8, 5))]
                    sh = 1
                    for lvl in range(g + 1):
                        dst, dkeys = bufs[lvl % 2]
                        lo = 2 * sh - 1
                        P.add("dve", lambda e, dst=dst, src=src, lo=lo, sh=sh: e.tensor_tensor(
                            out=dst[:, :, lo:528], in0=src[:, :, lo:528], in1=src[:, :, lo - sh:528 - sh], op=ALU.add),
                            reads=skeys, writes=dkeys)
                        src, skeys = dst, dkeys
                        sh *= 2
                    w = 2 ** (g + 1)
                    P.add("dve", lambda e, src=src, w=w, dg=dg: e.scalar_tensor_tensor(
                        out=dg[:, :, :], in0=src[:, :, 16:528], scalar=1.0 / w, in1=uG[:, :, 16:528],
                        op0=ALU.mult, op1=ALU.subtract),
                        reads=skeys + UK, writes=DK)
                    if tt == 0:
                        for c2 in range(2):
                            P.add("dve", lambda e, src=src, w=w, c2=c2: e.tensor_tensor(
                                out=src[:, c2, 16:16 + w - 1], in0=src[:, c2, 16:16 + w - 1], in1=invc[:, 0:w - 1],
                                op=ALU.mult), reads=skeys + ["invc"], writes=skeys)
                            P.add("dve", lambda e, src=src, w=w, c2=c2, dg=dg: e.tensor_tensor(
                                out=dg[:, c2, 0:w - 1], in0=src[:, c2, 16:16 + w - 1], in1=uG[:, c2, 16:16 + w - 1],
                                op=ALU.subtract), reads=skeys + UK, writes=DK)

                def pool_mm(g):
                    dg, DK = dbuf(g)
                    for oc in range(2):
                        b = nb()
                        for k2 in range(2):
                            P.add("pe", lambda e, b=b, k2=k2, oc=oc, g=g, dg=dg: e.matmul(
                                psum[:, b, :], pwv[:, g * 2 + k2, oc * 128:(oc + 1) * 128], dg[:, k2, :],
                                start=(k2 == 0), stop=(k2 == 1)),
                                reads=["wpool"] + DK, writes=[("ps", b)])
                        tick()
                        yc = 8 + 2 * g + oc
                        evac_copy(yT[:, yc, :], psum[:, b, :], [("ps", b)], pg(8 + yc, 1),
                                  scale=cst[:, PSC + 2 * g + oc:PSC + 2 * g + oc + 1])

                u_block(0)
                q_blocks()
                u_block(1)
                pool_mm(0)
                k_blocks()
                u_block(2)
                pool_mm(1)
                make_rT()
                v_blocks(8, 10)
                u_block(3)
                pool_mm(2)
                v_blocks(10, 12)
                pool_mm(3)
                PTC = pg(24, 5)
                YK = pg(30, 4)
                iters = [(qb, h) for qb in range(4) for h in range(16)]
                NIT = len(iters)

                def kslice(qb, j):
                    kb = qb - 4 + j
                    if kb >= 0:
                        return cur, kb
                    return prv, kb + 4

                def jmin_of(qb):
                    return max(0, 4 - qb) if tt == 0 else 0

                def S(it):
                    qb, h = iters[it]
                    ch, pb = h // 2, (h % 2) * 64
                    bm, bl = 2 * (it % 2), 2 * (it % 2) + 1
                    for j in range(jmin_of(qb), 5):
                        hf, kb = kslice(qb, j)
                        rd = []
                        if j < 4:
                            o_ap, wr = psum[:, bm, j * 128:(j + 1) * 128], [("ps", bm)]
                        else:
                            o_ap, wr = psum[:, bl, 0:128], [("ps", bl)]
                        P.add("pe", lambda e, o_ap=o_ap, hf=hf, kb=kb, ch=ch, pb=pb, qb=qb: e.matmul(
                            o_ap, KT[pb:pb + 64, ch, hf, kb * 128:(kb + 1) * 128],
                            QT[pb:pb + 64, ch, qb * 128:(qb + 1) * 128], start=True, stop=True),
                            reads=[("KT", ch, hf)] + pg(ch, 1) + rd, writes=wr)

                def X(it):
                    qb, h = iters[it]
                    bm, bl, i = 2 * (it % 2), 2 * (it % 2) + 1, it % 4
                    pt = PT[i]
                    jm = jmin_of(qb)
                    kf, k3, k4 = ("pt", i, "f"), ("pt", i, "3"), ("pt", i, "4")
                    if jm < 4:
                        P.add("act", lambda e, pt=pt, bm=bm, jm=jm: e.activation(
                            out=pt[:, jm * 128:512], in_=psum[:, bm, jm * 128:512], func=AF.Exp, scale=0.125),
                            reads=[("ps", bm)] + PTC, writes=([kf] if jm < 3 else []) + [k3])
                    P.add("act", lambda e, pt=pt, bl=bl: e.activation(
                        out=pt[:, 512:640], in_=psum[:, bl, 0:128], func=AF.Exp, scale=0.125),
                        reads=[("ps", bl)] + PTC, writes=[k4])
                    if jm == 0:
                        P.add("dve", lambda e, pt=pt: e.memset(pt[0:64, 64:128], 0.0), reads=PTC, writes=[kf])
                    lo = max(jm, 3)
                    P.add("dve", lambda e, pt=pt, h=h, lo=lo: e.tensor_tensor(
                        out=pt[:, lo * 128:640], in0=pt[:, lo * 128:640], in1=E[:, h, (lo - 3) * 128:256], op=ALU.mult),
                        reads=[("E", h // 4)] + PTC, writes=([k3] if lo == 3 else []) + [k4])

                def PV(it):
                    qb, h = iters[it]
                    i = it % 4
                    pt = PT[i]
                    jm = jmin_of(qb)
                    bo, oo = 4 + h // 7, (h % 7) * 65
                    for j in range(jm, 5):
                        hf, kb = kslice(qb, j)
                        kk = ("pt", i, "f") if j < 3 else ("pt", i, str(j))
                        P.add("pe", lambda e, bo=bo, oo=oo, pt=pt, j=j, hf=hf, kb=kb, h=h, jm=jm: e.matmul(
                            psum[:, bo, oo:oo + 65], pt[:, j * 128:(j + 1) * 128], V[:, hf * 4 + kb, h, :],
                            start=(j == jm), stop=(j == 4)),
                            reads=[kk, ("V", hf * 4 + kb)] + PTC, writes=[("ps", bo)])

                def NORM(bo, h0, nh):
                    ov = psum[:, bo, 0:nh * 65].rearrange("p (h d) -> p h d", d=65)
                    P.add("dve", lambda e, ov=ov, h0=h0, nh=nh: e.reciprocal(
                        rcp[:, h0:h0 + nh].unsqueeze(2), ov[:, :, 64:65]),
                        reads=[("ps", bo)], writes=[("rcp", bo)])
                    for hh in range(nh):
                        h = h0 + hh
                        P.add("dve", lambda e, ov=ov, hh=hh, h=h: e.tensor_scalar(
                            out=ytok[:, h * 64:(h + 1) * 64], in0=ov[:, hh, 0:64], scalar1=rcp[:, h:h + 1],
                            scalar2=None, op0=ALU.mult),
                            reads=[("ps", bo), ("rcp", bo)] + YK, writes=[("yt", h)])

                def TRp(qb, c4):
                    for ci in range(4):
                        c = c4 * 4 + ci
                        P.add("pe", lambda e, ci=ci, c=c: e.transpose(
                            psum[:, 7, ci * 128:(ci + 1) * 128], ytok[:, c * 128:(c + 1) * 128], ident[:]),
                            reads=YK + ["ident", ("yt", 2 * c), ("yt", 2 * c + 1)], writes=[("ps", 7)])

                def TRe(qb, c4):
                    evac_copy(yT[:, c4 * 4:(c4 + 1) * 4, qb * 128:(qb + 1) * 128],
                              psum[:, 7, :].rearrange("p (c t) -> p c t", t=128),
                              [("ps", 7)], pg(8 + c4 * 4, 4))

                for it in range(min(2, NIT)):
                    S(it)
                late = []
                for it in range(NIT):
                    qb, h = iters[it]
                    X(it)
                    for fn in late:
                        fn()
                    late[:] = []
                    PV(it)
                    if it + 2 < NIT:
                        S(it + 2)
                    if h == 6:
                        late.append(lambda: NORM(4, 0, 7))
                    elif h == 13:
                        late.append(lambda: NORM(5, 7, 7))
                    elif h == 15:
                        late.append(lambda: NORM(6, 14, 2))
                    if qb > 0 and h in (2, 4):
                        c4 = 0 if h == 2 else 1
                        TRp(qb - 1, c4)
                        late.append(lambda qb=qb, c4=c4: TRe(qb - 1, c4))
                for fn in late:
                    fn()
                late[:] = []
                TRp(3, 0)
                TRe(3, 0)
                TRp(3, 1)
                TRe(3, 1)
                proj_fm("w_out", 8, lambda kc: yT[:, kc, :], lambda kc: pg(8 + kc, 1), KC,
                        lambda dc, b: resid_add(dc, b, xn_for="cross"))

            def cross(ti):
                qblk = [ws.get("w_cq", 0), ws.get("w_cq", 1)]
                qbank = [nb() for _ in range(4)]
                for kc in range(KC):
                    for c in range(4):
                        sl, wv = qblk[c // 2]
                        P.add("pe", lambda e, b=qbank[c], wv=wv, kc=kc, c2=c % 2: e.matmul(
                            psum[:, b, :], wv[:, kc, c2 * 128:(c2 + 1) * 128], xnT[:, kc, :],
                            start=(kc == 0), stop=(kc == KC - 1)),
                            reads=[("ws", sl), ("xn", kc)], writes=[("ps", qbank[c])])
                    if kc == 3:
                        norm_stats("cross")
                tick()
                for c in range(4):
                    P.add("dve", lambda e, c=c: e.tensor_tensor(
                        out=QcT[:, c, :], in0=psum[:, qbank[c], :], in1=rstd[:], op=ALU.mult),
                        reads=[("ps", qbank[c]), "rstd"], writes=pg(c, 1))
                sc = float(128 ** -0.5)

                def CS(h):
                    i = h % 2
                    pc = PTc[i]
                    for mb in range(2):
                        b = nb()
                        P.add("pe", lambda e, b=b, h=h, mb=mb: e.matmul(
                            psum[:, b, :], KcT[:, h, mb * 128:(mb + 1) * 128], QcT[:, h, :], start=True, stop=True),
                            reads=["KcT"] + pg(h, 1), writes=[("ps", b)])
                        P.add("act", lambda e, b=b, pc=pc, mb=mb: e.activation(
                            out=pc[:, mb, :], in_=psum[:, b, :], func=AF.Exp, scale=sc),
                            reads=[("ps", b)], writes=pg(4 + 2 * i + mb, 1))

                def CV(h):
                    i = h % 2
                    pc, pck = PTc[i], pg(4 + 2 * i, 2)
                    bd, bv = nb(), nb()
                    for mb in range(2):
                        P.add("pe", lambda e, bd=bd, pc=pc, mb=mb: e.matmul(
                            psum[:, bd, :], ones[:], pc[:, mb, :], start=(mb == 0), stop=(mb == 1)),
                            reads=pck + ["ones"], writes=[("ps", bd)])
                    for mb in range(2):
                        P.add("pe", lambda e, bv=bv, pc=pc, mb=mb, h=h: e.matmul(
                            psum[:, bv, :], Vc[:, mb, h * 128:(h + 1) * 128], pc[:, mb, :],
                            start=(mb == 0), stop=(mb == 1)),
                            reads=pck + ["Vc"], writes=[("ps", bv)])
                    P.add("dve", lambda e, bd=bd: e.reciprocal(rden, psum[:, bd, :]),
                          reads=[("ps", bd)], writes=pg(8, 2))
                    P.add("dve", lambda e, bv=bv, h=h: e.tensor_tensor(
                        out=ocT[:, h, :], in0=psum[:, bv, :], in1=rden, op=ALU.mult),
                        reads=[("ps", bv)] + pg(8, 2), writes=pg(10 + h, 1))

                CS(0)
                CS(1)
                for h in range(4):
                    CV(h)
                    if h + 2 < 4:
                        CS(h + 2)
                for bi in range(2):
                    sl, wv = ws.get("w_co", bi)
                    c0 = 0
                    if bi == 0:
                        cb = [nb() for _ in range(4)]
                        for kc in range(4):
                            for c in range(4):
                                P.add("pe", lambda e, b=cb[c], wv=wv, kc=kc, c=c: e.matmul(
                                    psum[:, b, :], wv[:, kc, c * 128:(c + 1) * 128], ocT[:, kc, :],
                                    start=(kc == 0), stop=(kc == 3)),
                                    reads=[("ws", sl)] + pg(10 + kc, 1), writes=[("ps", cb[c])])
                        tick()
                        for c in range(4):
                            resid_add(c, cb[c], xn_for="ffn2")
                        c0 = 4
                    for c2 in range(c0, 8):
                        b = nb()
                        for kc in range(4):
                            P.add("pe", lambda e, b=b, wv=wv, kc=kc, c2=c2: e.matmul(
                                psum[:, b, :], wv[:, kc, c2 * 128:(c2 + 1) * 128], ocT[:, kc, :],
                                start=(kc == 0), stop=(kc == 3)),
                                reads=[("ws", sl)] + pg(10 + kc, 1), writes=[("ps", b)])
                        tick()
                        resid_add(bi * 8 + c2, b, xn_for="ffn2")

            def final(ti, nxt):
                for tb in range(4):
                    ost = ar_f32(8 * tb, 8)
                    obank = [nb() for _ in range(4)]
                    for c4 in range(4):
                        for ci in range(4):
                            c = c4 * 4 + ci
                            P.add("pe", lambda e, b=obank[c4], ci=ci, c=c, tb=tb: e.transpose(
                                psum[:, b, ci * 128:(ci + 1) * 128], hT[:, c, tb * 128:(tb + 1) * 128], ident[:]),
                                reads=[("hT", c, tb), "ident"], writes=[("ps", obank[c4])])
                        tick()
                    early_x = (tb == 0 and nxt)
                    if tb == 0:
                        norm_stats("final")
                        hold.update(obank)
                        if early_x:
                            xpose_x(ti + 1, tb)
                        make_rT()
                        hold.clear()
                    for c4 in range(4):
                        evac_copy(ost[:, c4 * 512:(c4 + 1) * 512], psum[:, obank[c4], :],
                                  [("ps", obank[c4]), "rT"], pg(8 * tb + 2 * c4, 2), scale=rT[:, tb:tb + 1])
                    r0 = ti * T + tb * 128
                    P.add("pool", lambda e, ost=ost, r0=r0: e.dma_start(out=out_d[r0:r0 + 128, :], in_=ost),
                          reads=pg(8 * tb, 8), lane="o%d" % (tb % 2))
                    if nxt:
                        if not early_x:
                            xpose_x(ti + 1, tb)
                        if tb + 2 < 4:
                            issue_x(ti + 1, tb + 2)

            issue_x(0, 0)
            issue_x(0, 1)
            xpose_x(0, 0)
            issue_x(0, 2)
            xpose_x(0, 1)
            issue_x(0, 3)
            seq_loads(0)
            ws.emit_casts()
            P.add("sp", lambda e: e.dma_start(out=wpool_sb[:].rearrange("p k n -> p (k n)"), in_=scr["w_pool"][0]),
                  reads=[("scr", "w_pool", 0)], writes=["wpool"], lane="wp")
            xpose_x(0, 2)
            xpose_x(0, 3)
            for ti in range(ntiles):
                if ti % TPS == 0:
                    if ti > 0:
                        seq_loads(ti // TPS)
                    seq_start(ti // TPS)
                ffn("ffn1", "mix")
                mixer(ti)
                cross(ti)
                nxt = ti + 1 < ntiles
                if nxt:
                    issue_x(ti + 1, 0)
                    issue_x(ti + 1, 1)
                ffn("ffn2", "final")
                final(ti, nxt)

        class WS:
            def __init__(self, P, plan=None):
                self.P = P
                self.plan = plan
                self.rec = []
                self.i = 0
                self.loaded = 0
                self.seen = set()

            def view(self, name, slot):
                _, kc, _, nbw = wblocks[name][0]
                return wsl[slot][:, 0:kc * nbw].rearrange("p (k n) -> p k n", n=nbw)

            def emit_casts(self):
                if self.plan is None:
                    return
                todo = [("w_pool", 0)]
                if not DIRECT:
                    for nb_ in self.plan:
                        if nb_ not in todo:
                            todo.append(nb_)
                for n, (name, bi) in enumerate(todo):
                    self.seen.add((name, bi))
                    r0, kc, c0, nbw = wblocks[name][bi]
                    src = W[name][r0:r0 + kc * 128, c0:c0 + nbw].rearrange("(k p) n -> p k n", p=128)
                    dst = scr[name][bi].rearrange("p (k n) -> p k n", n=nbw)
                    self.P.add("pool", lambda e, src=src, dst=dst: e.dma_start(out=dst, in_=src),
                               writes=[("scr", name, bi)], lane="c%d" % (n % 8))

            def _load(self, k):
                name, bi = self.plan[k]
                slot = k % NSLOT
                r0, kc, c0, nbw = wblocks[name][bi]
                if (name, bi) not in self.seen:
                    self.seen.add((name, bi))
                    src = W[name][r0:r0 + kc * 128, c0:c0 + nbw].rearrange("(k p) n -> p k n", p=128)
                    dstv = wsl[slot][:, 0:kc * nbw].rearrange("p (k n) -> p k n", n=nbw)
                    self.P.add("pool", lambda e, src=src, dstv=dstv: e.dma_start(out=dstv, in_=src),
                               writes=[("ws", slot)], lane="v%d" % slot)
                    self.P.add("sp", lambda e, name=name, bi=bi, slot=slot, kc=kc, nbw=nbw: e.dma_start(
                        out=scr[name][bi], in_=wsl[slot][:, 0:kc * nbw]),
                        reads=[("ws", slot)], writes=[("scr", name, bi)], lane="s%d" % slot)
                else:
                    self.P.add("sp", lambda e, name=name, bi=bi, slot=slot, kc=kc, nbw=nbw: e.dma_start(
                        out=wsl[slot][:, 0:kc * nbw], in_=scr[name][bi]),
                        reads=[("scr", name, bi)], writes=[("ws", slot)], lane="w%d" % slot)

            def get(self, name, bi):
                k = self.i
                self.i += 1
                if self.plan is None:
                    self.rec.append((name, bi))
                    return 0, self.view(name, 0)
                assert self.plan[k] == (name, bi)
                while self.loaded < min(k + NSLOT - 2, len(self.plan) - 1) + 1:
                    self._load(self.loaded)
                    self.loaded += 1
                return k % NSLOT, self.view(name, k % NSLOT)

        dryP = Prog(dry=True)
        dws = WS(dryP)
        emit_all(dryP, dws)
        P = Prog()
        ws = WS(P, plan=dws.rec)
        emit_all(P, ws)
        P.emit(nc, ctx)
    return nc


def _host_tables(rel_bias, norms, pool_scale):
    cst = np.zeros((128, 128), np.float32)
    for i, g in enumerate(norms):
        cst[:, 16 * i:16 * i + 16] = np.asarray(g, np.float32).reshape(16, 128).T
    cst[:, 96:104] = np.asarray(pool_scale, np.float32).reshape(8, 128).T
    rb = np.asarray(rel_bias, np.float32).reshape(16, 257)
    cst[:, 104:120] = rb[:, 256][None, :]
    cst[:, 127] = EPS
    k = np.arange(128)[:, None]
    q = np.arange(128)[None, :]
    idx3 = np.clip(128 + q - k, -128, 128) + 128
    idx4 = np.clip(q - k, -128, 128) + 128
    bm = np.stack([rb[:, idx3], rb[:, idx4]], axis=2)
    bm = np.ascontiguousarray(bm.transpose(1, 0, 2, 3)).reshape(128, 16 * 256)
    return cst, bm


_NC_CACHE = {}


def kernel(x, mem, ffn1_norm, ffn1_w_gate, ffn1_w_up, ffn1_w_down, mix_norm, w_in, rel_bias, w_pool,
           pool_scale, w_out, cross_norm, mem_norm, w_cq, w_ckv, w_co, ffn2_norm, ffn2_w_gate,
           ffn2_w_up, ffn2_w_down, final_norm):
    f = lambda a: np.ascontiguousarray(np.asarray(a, dtype=np.float32))
    x = f(x)
    mem = f(mem)
    cst, bm = _host_tables(rel_bias, [f(ffn1_norm)[0], f(mix_norm)[0], f(cross_norm)[0], f(ffn2_norm)[0],
                                      f(final_norm), f(mem_norm)[0]], f(pool_scale)[0])
    shared = {
        "cst": cst, "bmat": bm,
        "ffn1_g": f(ffn1_w_gate)[0], "ffn1_u": f(ffn1_w_up)[0], "ffn1_d": f(ffn1_w_down)[0],
        "ffn2_g": f(ffn2_w_gate)[0], "ffn2_u": f(ffn2_w_up)[0], "ffn2_d": f(ffn2_w_down)[0],
        "w_in": f(w_in)[0], "w_pool": f(w_pool)[0].reshape(1024, 256), "w_out": f(w_out)[0],
        "w_cq": f(w_cq)[0], "w_ckv": f(w_ckv)[0], "w_co": f(w_co)[0],
    }
    if "nc" not in _NC_CACHE:
        _NC_CACHE["nc"] = build_program()
    nc = _NC_CACHE["nc"]
    in_maps = []
    for c in range(NCORES):
        m = dict(shared)
        m["x"] = x[2 * c:2 * c + 2].reshape(NSEQ * SEQ, D)
        m["mem"] = mem[2 * c:2 * c + 2].reshape(NSEQ * NMEM, D)
        in_maps.append(m)
    res = run_bass_kernel_spmd(nc, in_maps, core_ids=list(range(NCORES)))
    out = np.stack([np.asarray(r["out"], dtype=np.float32).reshape(NSEQ, SEQ, D) for r in res.results], axis=0)
    return out.reshape(16, SEQ, D)
```

```python
import contextlib
import numpy as np
import concourse.bass as bass
import concourse.mybir as mybir
from concourse.bass_utils import run_bass_kernel_spmd

F32 = mybir.dt.float32
BF16 = mybir.dt.bfloat16
I32 = mybir.dt.int32
AF = mybir.ActivationFunctionType
ALU = mybir.AluOpType

NCORES = 8
D = 2048
DFF = 5632
T = 512
SEQ = 2048
TPS = SEQ // T
NSEQ = 2
NMEM = 256
KC = D // 128
EPS = 1e-6
NSLOT = 6
SLOTW = 4096
NPAGE = 34
import os
PRUNE = os.environ.get('K_PRUNE', '1') == '1'
DIRECT = os.environ.get('K_DIRECT', '0') == '1'


class Op:
    __slots__ = ("eng", "fn", "deps", "lane", "signal", "semval")

    def __init__(self, eng, fn, deps, lane):
        self.eng, self.fn, self.deps, self.lane = eng, fn, deps, lane
        self.signal = False
        self.semval = None


class Prog:
    ENGS = ("pe", "act", "dve", "pool", "sp")

    def __init__(self, dry=False):
        self.ops = []
        self.lastw = {}
        self.readers = {}
        self.dry = dry

    def add(self, eng, fn, reads=(), writes=(), lane=None):
        if self.dry:
            return
        i = len(self.ops)
        writes = list(writes)
        if lane is not None:
            writes.append(("lane", lane))
        deps = set()
        for r in reads:
            w = self.lastw.get(r)
            if w is not None:
                deps.add(w)
        for r in writes:
            w = self.lastw.get(r)
            if w is not None:
                deps.add(w)
            rs = self.readers.get(r)
            if rs:
                deps.update(rs)
        for r in reads:
            self.readers.setdefault(r, []).append(i)
        for r in writes:
            self.lastw[r] = i
            self.readers[r] = []
        deps.discard(i)
        self.ops.append(Op(eng, fn, deps, lane))

    def emit(self, nc, ctx):
        ops = self.ops
        for o in ops:
            if o.eng == "pe":
                o.deps = {d for d in o.deps if ops[d].eng != "pe"}
            if not PRUNE:
                continue
            best = {}
            keep = set()
            for d in o.deps:
                pe_ = ops[d]
                if pe_.lane is not None:
                    keep.add(d)
                elif best.get(pe_.eng, -1) < d:
                    best[pe_.eng] = d
            keep.update(best.values())
            o.deps = keep
        for o in ops:
            for d in o.deps:
                ops[d].signal = True
        lanes = sorted({o.lane for o in ops if o.lane is not None})
        sems = {}
        for e in ("pe", "act", "dve", "pool"):
            sems[e] = ctx.enter_context(nc.semaphore("s_" + e))
        for l in lanes:
            sems[("lane", l)] = ctx.enter_context(nc.semaphore("l_%s" % (l,)))
        cnt = {k: 0 for k in sems}
        for o in ops:
            if o.lane is not None:
                k = ("lane", o.lane)
                cnt[k] += 16
                o.semval = (k, cnt[k])
            elif o.signal:
                cnt[o.eng] += 1
                o.semval = (o.eng, cnt[o.eng])
        streams = {e: [o for o in ops if o.eng == e] for e in self.ENGS}
        block = ctx.enter_context(nc.Block())

        def run(stream, e):
            waited = {}
            for o in stream:
                need = {}
                for d in o.deps:
                    k, v = ops[d].semval
                    if need.get(k, 0) < v:
                        need[k] = v
                for k, v in need.items():
                    if waited.get(k, 0) < v:
                        e.wait_ge(sems[k], v)
                        waited[k] = v
                ins = o.fn(e)
                if o.semval is not None:
                    ins.then_inc(sems[o.semval[0]], 16 if o.lane is not None else 1)

        @block.tensor
        def _(e):
            run(streams["pe"], e)

        @block.scalar
        def _(e):
            run(streams["act"], e)

        @block.vector
        def _(e):
            run(streams["dve"], e)

        @block.gpsimd
        def _(e):
            run(streams["pool"], e)
            for l in lanes:
                if str(l).startswith("o"):
                    e.wait_ge(sems[("lane", l)], cnt[("lane", l)])

        @block.sync
        def _(e):
            run(streams["sp"], e)


def build_program(ntiles=NSEQ * TPS):
    nc = bass.Bass("TRN2", target_bir_lowering=False)
    ntok = ntiles * T
    dr = lambda name, shape, dt=F32, kind="ExternalInput": nc.dram_tensor(name, shape, dt, kind=kind).ap()
    x_d = dr("x", [NSEQ * SEQ, D])
    mem_d = dr("mem", [NSEQ * NMEM, D])
    out_d = dr("out", [NSEQ * SEQ, D], kind="ExternalOutput")
    cst_d = dr("cst", [128, 128])
    bm_d = dr("bmat", [128, 16 * 256])
    W = {}
    for f in ("ffn1", "ffn2"):
        W[f + "_g"] = dr(f + "_g", [D, DFF])
        W[f + "_u"] = dr(f + "_u", [D, DFF])
        W[f + "_d"] = dr(f + "_d", [DFF, D])
    W["w_in"] = dr("w_in", [D, 4096])
    W["w_pool"] = dr("w_pool", [1024, 256])
    W["w_out"] = dr("w_out", [D, D])
    W["w_cq"] = dr("w_cq", [D, 512])
    W["w_ckv"] = dr("w_ckv", [D, 1024])
    W["w_co"] = dr("w_co", [512, D])

    wblocks = {}
    for f in ("ffn1", "ffn2"):
        wblocks[f + "_g"] = [(0, 16, j * 256, 256) for j in range(22)]
        wblocks[f + "_u"] = [(0, 16, j * 256, 256) for j in range(22)]
        wblocks[f + "_d"] = [((hh * 22 + rg * 11) * 128, 11, cb * 256, 256)
                             for hh in range(2) for rg in range(2) for cb in range(8)]
    wblocks["w_in"] = [(0, 16, j * 256, 256) for j in range(16)]
    wblocks["w_pool"] = [(0, 8, 0, 256)]
    wblocks["w_out"] = [(0, 16, j * 256, 256) for j in range(8)]
    wblocks["w_cq"] = [(0, 16, j * 256, 256) for j in range(2)]
    wblocks["w_ckv"] = [(0, 16, j * 256, 256) for j in range(4)]
    wblocks["w_co"] = [(0, 4, j * 1024, 1024) for j in range(2)]
    scr = {}
    for name, bl in wblocks.items():
        scr[name] = nc.dram_tensor("scr_" + name, [len(bl), 128, bl[0][1] * bl[0][3]], BF16).ap()

    with contextlib.ExitStack() as ctx:
        sb = lambda name, shape, dt: ctx.enter_context(nc.sbuf_tensor(name, shape, dt))
        hT = sb("hT", [128, KC, T], F32)
        xnT = sb("xnT", [128, KC, T], BF16)
        arena = sb("arena", [128, NPAGE * 512], BF16)
        KT = sb("KT", [128, 8, 2, T], BF16)
        V = sb("V", [128, 8, 16, 65], BF16)
        E = sb("E", [128, 16, 256], BF16)
        xst = [sb("xst%d" % i, [128, D], F32) for i in range(2)]
        wsl = [sb("wsl%d" % i, [128, SLOTW], BF16) for i in range(NSLOT)]
        KcT = sb("KcT", [128, 4, NMEM], BF16)
        Vc = sb("Vc", [128, 2, 512], BF16)
        sq = [sb("sq%d" % i, [128, T], BF16) for i in range(3)]
        dG1 = sb("dG1", [128, 2, T], BF16)
        rstd = sb("rstd", [128, T], F32)
        sg = [sb("sg%d" % i, [128, T], F32) for i in range(2)]
        cst = sb("cst_sb", [128, 128], F32)
        ident = sb("ident", [128, 128], F32)
        ones = sb("ones", [128, 128], BF16)
        halo = sb("halo", [128, 8, 16], F32)
        invc = sb("invc", [128, 16], F32)
        invc_i = sb("invc_i", [128, 16], I32)
        rcp = sb("rcp", [128, 16], F32)
        mst = sb("mst", [128, 4], F32)
        nbf = sb("nbf", [128, 16], F32)
        rT = sb("rT", [128, 4], F32)
        wpool_sb = sb("wpool_sb", [128, 8, 256], BF16)
        psum = ctx.enter_context(nc.psum_tensor("psum", [128, 8, 512], F32))

        GC = {"ffn1": 0, "mix": 16, "cross": 32, "ffn2": 48, "final": 64, "mem": 80}
        PSC = 96
        BFAR = 104

        def pg(p0, npg):
            return [("ar", p) for p in range(p0, p0 + npg)]

        def ar_bf(p0, npages):
            return arena[:, p0 * 512:(p0 + npages) * 512]

        def ar_f32(p0, npages):
            return arena[:, p0 * 512:(p0 + npages) * 512].bitcast(F32)

        hid = ar_bf(0, 22).rearrange("p (c t) -> p c t", t=T)
        QT = ar_bf(0, 8).rearrange("p (c t) -> p c t", t=T)
        yT = ar_bf(8, 16).rearrange("p (c t) -> p c t", t=T)
        uG = ar_f32(24, 5)[:, 0:1056].rearrange("p (c t) -> p c t", t=528)
        tA = ar_f32(29, 5)[:, 0:1056].rearrange("p (c t) -> p c t", t=528)
        tB = ar_f32(8, 5)[:, 0:1056].rearrange("p (c t) -> p c t", t=528)
        dG = ar_bf(13, 2).rearrange("p (c t) -> p c t", t=T)
        PT = [arena[:, 24 * 512 + i * 640:24 * 512 + (i + 1) * 640] for i in range(4)]
        ytok = ar_f32(30, 4)
        QcT = ar_bf(0, 4).rearrange("p (c t) -> p c t", t=T)
        PTc = [ar_bf(4 + 2 * i, 2).rearrange("p (c t) -> p c t", t=T) for i in range(2)]
        rden = ar_f32(8, 2)
        ocT = ar_bf(10, 4).rearrange("p (c t) -> p c t", t=T)
        memst = [ar_f32(8 * i, 8) for i in range(2)]
        mnT = ar_bf(16, 8).rearrange("p (c t) -> p c t", t=NMEM)
        bmst = ar_f32(0, 16)

        hk = lambda c: [("hT", c, tb) for tb in range(4)]

        def emit_all(P, ws):
            bank_ctr = [0]

            hold = set()

            def nb():
                while True:
                    b = bank_ctr[0] % 7
                    bank_ctr[0] += 1
                    if b not in hold:
                        return b

            SB = 7
            pend = []
            stat_n = [0]

            def tick():
                keep = []
                for ent in pend:
                    if ent[0] >= 1:
                        ent[1]()
                    else:
                        ent[0] += 1
                        keep.append(ent)
                pend[:] = keep

            def flush():
                for ent in pend:
                    ent[1]()
                pend[:] = []

            def ensure_free(tag):
                last = -1
                for i, ent in enumerate(pend):
                    if ent[2] == tag:
                        last = i
                for ent in pend[:last + 1]:
                    ent[1]()
                pend[:] = pend[last + 1:]

            def stat_hook(dc):
                k = stat_n[0]
                stat_n[0] += 1
                sbuf = sq[k % 3]
                ensure_free(k % 3)
                P.add("act", lambda e, dc=dc, sbuf=sbuf: e.activation(out=sbuf[:], in_=hT[:, dc, :], func=AF.Square),
                      reads=hk(dc), writes=[("sq", k % 3)])

                def mm(k=k, sbuf=sbuf):
                    P.add("pe", lambda e: e.matmul(psum[:, SB, :], ones[:], sbuf[:], start=(k == 0), stop=(k == KC - 1)),
                          reads=[("sq", k % 3), "ones"], writes=[("ps", SB)])
                pend.append([0, mm, k % 3])

            evq = [0]

            P.add("pool", lambda e: e.dma_start(out=cst[:], in_=cst_d), writes=["cst"], lane="x0")
            P.add("pool", lambda e: e.dma_start(out=bmst, in_=bm_d), writes=pg(0, 16), lane="x1")
            P.add("pool", lambda e: e.memset(ident[:], 0.0), writes=["ident"])
            P.add("pool", lambda e: e.affine_select(out=ident[:], in_=ident[:], pattern=[[-1, 128]],
                                                    compare_op=ALU.not_equal, fill=1.0, base=0,
                                                    channel_multiplier=1),
                  reads=["ident"], writes=["ident"])
            P.add("pool", lambda e: e.memset(ones[:], 1.0), writes=["ones"])
            P.add("pool", lambda e: e.memset(V[:], 1.0), writes=[("V", i) for i in range(8)])
            P.add("pool", lambda e: e.iota(invc_i[:], [[1, 16]], base=1, channel_multiplier=0),
                  writes=["invc_i"])
            P.add("dve", lambda e: e.tensor_copy(invc[:], invc_i[:]), reads=["invc_i"], writes=["invc"])
            P.add("dve", lambda e: e.reciprocal(invc[:], invc[:]), reads=["invc"], writes=["invc"])
            P.add("act", lambda e: e.activation(out=nbf[:], in_=cst[:, BFAR:BFAR + 16], func=AF.Copy, scale=-1.0),
                  reads=["cst"], writes=["nbf"])
            for h in range(16):
                P.add("act", lambda e, h=h: e.activation(
                    out=E[:, h, :], in_=bmst[:, h * 256:(h + 1) * 256], func=AF.Exp, bias=nbf[:, h:h + 1]),
                    reads=pg(h, 1) + ["nbf"], writes=[("E", h // 4)])
            P.add("dve", lambda e: e.memset(E[64:128, :, 128:192], 0.0),
                  reads=[("E", i) for i in range(4)], writes=[("E", i) for i in range(4)])

            def evac_copy(out, in_, reads, writes, scale=None):
                evq[0] += 1
                if scale is not None:
                    P.add("act", lambda e: e.activation(out=out, in_=in_, func=AF.Copy, scale=scale),
                          reads=reads, writes=writes)
                elif evq[0] % 2:
                    P.add("act", lambda e: e.activation(out=out, in_=in_, func=AF.Copy),
                          reads=reads, writes=writes)
                else:
                    P.add("dve", lambda e: e.tensor_copy(out, in_), reads=reads, writes=writes)

            def norm_stats(gname):
                flush()
                assert stat_n[0] == KC, stat_n[0]
                stat_n[0] = 0
                P.add("act", lambda e: e.activation(out=rstd[:], in_=psum[:, SB, :], func=AF.Sqrt,
                                                    scale=1.0 / D, bias=cst[:, 127:128]),
                      reads=[("ps", SB), "cst"], writes=["rstd"])
                P.add("dve", lambda e: e.reciprocal(rstd[:], rstd[:]), reads=["rstd"], writes=["rstd"])

            def norm(gname):
                norm_stats(gname)
                g0 = GC[gname]
                for c in range(KC):
                    P.add("dve", lambda e, c=c: e.scalar_tensor_tensor(
                        out=xnT[:, c, :], in0=hT[:, c, :], scalar=cst[:, g0 + c:g0 + c + 1], in1=rstd[:],
                        op0=ALU.mult, op1=ALU.mult),
                        reads=hk(c) + ["rstd", "cst"], writes=[("xn", c)])

            XN = [("xn", c) for c in range(KC)]

            def ffn(f, nxt):
                ci = 0
                for hh in range(2):
                    for fbl in range(11):
                        fb = hh * 11 + fbl
                        gs, gv = ws.get(f + "_g", fb)
                        us, uv = ws.get(f + "_u", fb)
                        first = (hh == 0 and fbl == 0)
                        banks = [(nb(), nb()) for _ in range(2)]
                        if first:
                            for kc in range(KC):
                                for c2 in range(2):
                                    for (bk, wv, sl) in ((banks[c2][0], gv, gs), (banks[c2][1], uv, us)):
                                        P.add("pe", lambda e, bk=bk, wv=wv, kc=kc, c2=c2: e.matmul(
                                            psum[:, bk, :], wv[:, kc, c2 * 128:(c2 + 1) * 128], xnT[:, kc, :],
                                            start=(kc == 0), stop=(kc == KC - 1)),
                                            reads=[("ws", sl), ("xn", kc)], writes=[("ps", bk)])
                                if kc == 3:
                                    norm_stats(f)
                        for c2 in range(2):
                            lc = fbl * 2 + c2
                            bg, bu = banks[c2]
                            if not first:
                                for (bk, wv, sl) in ((bg, gv, gs), (bu, uv, us)):
                                    for kc in range(KC):
                                        P.add("pe", lambda e, bk=bk, wv=wv, kc=kc, c2=c2: e.matmul(
                                            psum[:, bk, :], wv[:, kc, c2 * 128:(c2 + 1) * 128], xnT[:, kc, :],
                                            start=(kc == 0), stop=(kc == KC - 1)),
                                            reads=[("ws", sl), ("xn", kc)], writes=[("ps", bk)])
                            tick()
                            s = sg[ci % 2]
                            sk = ("sg", ci % 2)
                            P.add("dve", lambda e, s=s, bg=bg: e.tensor_tensor(
                                out=s[:], in0=psum[:, bg, :], in1=rstd[:], op=ALU.mult),
                                reads=[("ps", bg), "rstd"], writes=[sk])
                            P.add("act", lambda e, s=s: e.activation(out=s[:], in_=s[:], func=AF.Silu),
                                  reads=[sk], writes=[sk])
                            P.add("dve", lambda e, s=s, bu=bu: e.tensor_tensor(
                                out=s[:], in0=s[:], in1=psum[:, bu, :], op=ALU.mult),
                                reads=[sk, ("ps", bu)], writes=[sk])
                            P.add("dve", lambda e, s=s, lc=lc: e.tensor_tensor(
                                out=hid[:, lc, :], in0=s[:], in1=rstd[:], op=ALU.mult),
                                reads=[sk, "rstd"], writes=pg(lc, 1))
                            ci += 1
                    for cb in range(8):
                        d0s, d0v = ws.get(f + "_d", (hh * 2 + 0) * 8 + cb)
                        d1s, d1v = ws.get(f + "_d", (hh * 2 + 1) * 8 + cb)
                        for d2 in range(2):
                            dc = cb * 2 + d2
                            b = nb()
                            for rg, (dv, ds) in enumerate(((d0v, d0s), (d1v, d1s))):
                                for fi in range(11):
                                    lc = rg * 11 + fi
                                    P.add("pe", lambda e, b=b, dv=dv, fi=fi, d2=d2, lc=lc, rg=rg: e.matmul(
                                        psum[:, b, :], dv[:, fi, d2 * 128:(d2 + 1) * 128], hid[:, lc, :],
                                        start=(rg == 0 and fi == 0), stop=(rg == 1 and fi == 10)),
                                        reads=[("ws", ds)] + pg(lc, 1), writes=[("ps", b)])
                            tick()
                            P.add("dve", lambda e, b=b, dc=dc: e.scalar_tensor_tensor(
                                out=hT[:, dc, :], in0=psum[:, b, :], scalar=0.5, in1=hT[:, dc, :],
                                op0=ALU.mult, op1=ALU.add),
                                reads=[("ps", b)] + hk(dc), writes=hk(dc))
                            if hh == 1:
                                stat_hook(dc)
                                if nxt == "mix":
                                    xn_plain(dc, "mix")
                                elif nxt == "final":
                                    hg_inplace(dc)

            def proj_fm(wname, nblk, rhs_of_kc, rhs_keys, nkc, consume, cols_per_blk=256):
                for bi in range(nblk):
                    s, wv = ws.get(wname, bi)
                    for c2 in range(cols_per_blk // 128):
                        b = nb()
                        for kc in range(nkc):
                            P.add("pe", lambda e, b=b, wv=wv, kc=kc, c2=c2: e.matmul(
                                psum[:, b, :], wv[:, kc, c2 * 128:(c2 + 1) * 128], rhs_of_kc(kc),
                                start=(kc == 0), stop=(kc == nkc - 1)),
                                reads=[("ws", s)] + rhs_keys(kc), writes=[("ps", b)])
                        tick()
                        consume(bi * (cols_per_blk // 128) + c2, b)

            def xn_plain(dc, gname):
                g0 = GC[gname]
                P.add("act", lambda e: e.activation(out=xnT[:, dc, :], in_=hT[:, dc, :], func=AF.Copy,
                                                    scale=cst[:, g0 + dc:g0 + dc + 1]),
                      reads=hk(dc) + ["cst"], writes=[("xn", dc)])

            def make_rT():
                b = nb()
                for tb in range(4):
                    P.add("pe", lambda e, b=b, tb=tb: e.transpose(
                        psum[:, b, tb * 128:(tb + 1) * 128], rstd[:, tb * 128:(tb + 1) * 128], ident[:]),
                        reads=["rstd", "ident"], writes=[("ps", b)])
                P.add("dve", lambda e, b=b: e.tensor_copy(
                    rT[:, :].unsqueeze(2), psum[:, b, :].rearrange("p (t c) -> p t c", c=128)[:, :, 0:1]),
                    reads=[("ps", b)], writes=["rT"])

            def hg_inplace(dc):
                g0 = GC["final"]
                P.add("act", lambda e: e.activation(out=hT[:, dc, :], in_=hT[:, dc, :], func=AF.Copy,
                                                    scale=cst[:, g0 + dc:g0 + dc + 1]),
                      reads=hk(dc) + ["cst"], writes=hk(dc))

            def resid_add(dc, b, xn_for=None):
                P.add("dve", lambda e: e.tensor_tensor(out=hT[:, dc, :], in0=psum[:, b, :], in1=hT[:, dc, :],
                                                       op=ALU.add),
                      reads=[("ps", b)] + hk(dc), writes=hk(dc))
                stat_hook(dc)
                if xn_for is not None:
                    xn_plain(dc, xn_for)

            def seq_loads(s):
                for mb in range(2):
                    ms = memst[mb]
                    r0 = s * NMEM + mb * 128
                    P.add("pool", lambda e, ms=ms, r0=r0: e.dma_start(out=ms, in_=mem_d[r0:r0 + 128, :]),
                          writes=pg(8 * mb, 8), lane="x%d" % mb)

            def seq_start(s):
                P.add("dve", lambda e: e.memset(halo[:], 0.0), writes=["halo"])
                for mb in range(2):
                    ms = memst[mb]
                    P.add("act", lambda e, ms=ms, mb=mb: e.activation(out=ar_bf(24, 4), in_=ms, func=AF.Square,
                                                                      accum_out=mst[:, mb:mb + 1]),
                          reads=pg(8 * mb, 8), writes=pg(24, 4) + [("mst", mb)])
                    P.add("act", lambda e, mb=mb: e.activation(out=mst[:, mb:mb + 1], in_=mst[:, mb:mb + 1], func=AF.Sqrt,
                                                               scale=1.0 / D, bias=cst[:, 127:128]),
                          reads=[("mst", mb), "cst"], writes=[("mst", mb)])
                    P.add("dve", lambda e, mb=mb: e.reciprocal(mst[:, mb:mb + 1], mst[:, mb:mb + 1]),
                          reads=[("mst", mb)], writes=[("mst", mb)])
                    P.add("dve", lambda e, ms=ms, mb=mb: e.tensor_scalar(out=ms, in0=ms, scalar1=mst[:, mb:mb + 1],
                                                                       scalar2=None, op0=ALU.mult),
                          reads=pg(8 * mb, 8) + [("mst", mb)], writes=pg(8 * mb, 8))
                    for c4 in range(4):
                        b = nb()
                        for ci in range(4):
                            c = c4 * 4 + ci
                            P.add("pe", lambda e, b=b, ci=ci, c=c, ms=ms: e.transpose(
                                psum[:, b, ci * 128:(ci + 1) * 128], ms[:, c * 128:(c + 1) * 128], ident[:]),
                                reads=pg(8 * mb, 8) + ["ident"], writes=[("ps", b)])
                        for ci in range(4):
                            c = c4 * 4 + ci
                            P.add("act", lambda e, b=b, ci=ci, c=c, mb=mb: e.activation(
                                out=mnT[:, c, mb * 128:(mb + 1) * 128], in_=psum[:, b, ci * 128:(ci + 1) * 128],
                                func=AF.Copy, scale=cst[:, GC["mem"] + c:GC["mem"] + c + 1]),
                                reads=[("ps", b), "cst"], writes=pg(16, 8))
                for bi in range(2):
                    sl, wv = ws.get("w_ckv", bi)
                    for c2 in range(2):
                        b = nb()
                        for kc in range(KC):
                            P.add("pe", lambda e, b=b, wv=wv, kc=kc, c2=c2: e.matmul(
                                psum[:, b, 0:NMEM], wv[:, kc, c2 * 128:(c2 + 1) * 128], mnT[:, kc, :],
                                start=(kc == 0), stop=(kc == KC - 1)),
                                reads=[("ws", sl)] + pg(16, 8), writes=[("ps", b)])
                        hh = bi * 2 + c2
                        evac_copy(KcT[:, hh, :], psum[:, b, 0:NMEM], [("ps", b)], ["KcT"])
                for bi in range(2, 4):
                    sl, wv = ws.get("w_ckv", bi)
                    for mb in range(2):
                        b = nb()
                        for kc in range(KC):
                            P.add("pe", lambda e, b=b, wv=wv, kc=kc, mb=mb: e.matmul(
                                psum[:, b, 0:256], mnT[:, kc, mb * 128:(mb + 1) * 128], wv[:, kc, :],
                                start=(kc == 0), stop=(kc == KC - 1)),
                                reads=[("ws", sl)] + pg(16, 8), writes=[("ps", b)])
                        evac_copy(Vc[:, mb, (bi - 2) * 256:(bi - 1) * 256], psum[:, b, 0:256], [("ps", b)], ["Vc"])

            def issue_x(ti, tb):
                xs = xst[tb % 2]
                r0 = ti * T + tb * 128
                P.add("pool", lambda e, xs=xs, r0=r0: e.dma_start(out=xs[:], in_=x_d[r0:r0 + 128, :]),
                      writes=[("xst", tb % 2)], lane="x%d" % (tb % 2))

            def xpose_x(ti, tb):
                xs = xst[tb % 2]
                for c4 in range(4):
                    b = nb()
                    for ci in range(4):
                        c = c4 * 4 + ci
                        P.add("pe", lambda e, b=b, ci=ci, c=c, xs=xs: e.transpose(
                            psum[:, b, ci * 128:(ci + 1) * 128], xs[:, c * 128:(c + 1) * 128], ident[:]),
                            reads=[("xst", tb % 2), "ident"], writes=[("ps", b)])
                    tick()
                    hkeys = [("hT", c4 * 4 + ci, tb) for ci in range(4)]
                    P.add("dve", lambda e, b=b, c4=c4: e.tensor_copy(
                        hT[:, c4 * 4:(c4 + 1) * 4, tb * 128:(tb + 1) * 128],
                        psum[:, b, :].rearrange("p (c t) -> p c t", t=128)),
                        reads=[("ps", b)], writes=hkeys)
                    k = c4
                    sbuf = sq[k % 3]
                    ensure_free(k % 3)
                    P.add("act", lambda e, c4=c4, sbuf=sbuf: e.activation(
                        out=sbuf[:].rearrange("p (c t) -> p c t", t=128),
                        in_=hT[:, c4 * 4:(c4 + 1) * 4, tb * 128:(tb + 1) * 128], func=AF.Square),
                        reads=hkeys, writes=[("sq", k % 3)])

                    def mm(c4=c4, sbuf=sbuf, k=k):
                        for ci in range(4):
                            P.add("pe", lambda e, ci=ci: e.matmul(
                                psum[:, SB, tb * 128:(tb + 1) * 128], ones[:], sbuf[:, ci * 128:(ci + 1) * 128],
                                start=(c4 == 0 and ci == 0), stop=(c4 == 3 and ci == 3)),
                                reads=[("sq", k % 3), "ones"], writes=[("ps", SB)])
                    pend.append([0, mm, k % 3])
                if tb == 3:
                    stat_n[0] = KC
                    g0 = GC["ffn1"]
                    for c in range(KC):
                        P.add("dve", lambda e, c=c: e.tensor_scalar(
                            out=xnT[:, c, :], in0=hT[:, c, :], scalar1=cst[:, g0 + c:g0 + c + 1], scalar2=None,
                            op0=ALU.mult),
                            reads=hk(c) + ["cst"], writes=[("xn", c)])

            def mixer(ti):
                tt = ti % TPS
                cur, prv = tt % 2, (tt + 1) % 2
                def q_blocks():
                    proj_fm("w_in", 4, lambda kc: xnT[:, kc, :], lambda kc: [("xn", kc)], KC,
                            lambda c, b: P.add("dve", lambda e: e.tensor_tensor(
                                out=QT[:, c, :], in0=psum[:, b, :], in1=rstd[:], op=ALU.mult),
                                reads=[("ps", b), "rstd"], writes=pg(c, 1)))

                def k_blocks():
                    for bi in range(4, 8):
                        sl, wv = ws.get("w_in", bi)
                        for c2 in range(2):
                            c = (bi - 4) * 2 + c2
                            b = nb()
                            for kc in range(KC):
                                P.add("pe", lambda e, b=b, wv=wv, kc=kc, c2=c2: e.matmul(
                                    psum[:, b, :], wv[:, kc, c2 * 128:(c2 + 1) * 128], xnT[:, kc, :],
                                    start=(kc == 0), stop=(kc == KC - 1)),
                                    reads=[("ws", sl), ("xn", kc)], writes=[("ps", b)])
                            tick()
                            P.add("dve", lambda e, b=b, c=c: e.tensor_tensor(
                                out=KT[:, c, cur, :], in0=psum[:, b, :], in1=rstd[:], op=ALU.mult),
                                reads=[("ps", b), "rstd"], writes=[("KT", c, cur)])

                def v_blocks(lo, hi):
                    for bi in range(lo, hi):
                        sl, wv = ws.get("w_in", bi)
                        vb = bi - 8
                        for tb in range(4):
                            b = nb()
                            for kc in range(KC):
                                P.add("pe", lambda e, b=b, wv=wv, kc=kc, tb=tb: e.matmul(
                                    psum[:, b, 0:256], xnT[:, kc, tb * 128:(tb + 1) * 128], wv[:, kc, :],
                                    start=(kc == 0), stop=(kc == KC - 1)),
                                    reads=[("ws", sl), ("xn", kc)], writes=[("ps", b)])
                            tick()
                            evac_copy(V[:, cur * 4 + tb, vb * 4:(vb + 1) * 4, 0:64],
                                      psum[:, b, 0:256].rearrange("p (h d) -> p h d", d=64),
                                      [("ps", b), "rT"], [("V", cur * 4 + tb)], scale=rT[:, tb:tb + 1])

                pwv = wpool_sb
                UK = pg(24, 5)

                def dbuf(g):
                    return (dG, pg(13, 2)) if g % 2 == 0 else (dG1, [("dg1",)])

                def u_block(g):
                    sl, wv = ws.get("w_in", 12 + g)
                    dg, DK = dbuf(g)
                    ub = [nb(), nb()]
                    if g == 0:
                        for kc in range(KC):
                            for c2 in range(2):
                                P.add("pe", lambda e, b=ub[c2], wv=wv, kc=kc, c2=c2: e.matmul(
                                    psum[:, b, :], wv[:, kc, c2 * 128:(c2 + 1) * 128], xnT[:, kc, :],
                                    start=(kc == 0), stop=(kc == KC - 1)),
                                    reads=[("ws", sl), ("xn", kc)], writes=[("ps", ub[c2])])
                            if kc == 7:
                                norm_stats("mix")
                    for c2 in range(2):
                        b = ub[c2]
                        if g != 0:
                            for kc in range(KC):
                                P.add("pe", lambda e, b=b, wv=wv, kc=kc, c2=c2: e.matmul(
                                    psum[:, b, :], wv[:, kc, c2 * 128:(c2 + 1) * 128], xnT[:, kc, :],
                                    start=(kc == 0), stop=(kc == KC - 1)),
                                    reads=[("ws", sl), ("xn", kc)], writes=[("ps", b)])
                        tick()
                        P.add("dve", lambda e, b=b, c2=c2: e.tensor_tensor(
                            out=uG[:, c2, 16:528], in0=psum[:, b, :], in1=rstd[:], op=ALU.mult),
                            reads=[("ps", b), "rstd"], writes=UK)
                    P.add("dve", lambda e, g=g: e.tensor_copy(uG[:, :, 0:16], halo[:, 2 * g:2 * g + 2, :]),
                          reads=["halo"], writes=UK)
                    P.add("dve", lambda e, g=g: e.tensor_copy(halo[:, 2 * g:2 * g + 2, :], uG[:, :, 512:528]),
                          reads=UK, writes=["halo"])
                    src, skeys = uG, UK
                    bufs = [(tA, pg(29, 5)), (tB, pg(8, 5))]
                    sh = 1
                    for lvl in range(g + 1):
                        dst, dkeys = bufs[lvl % 2]
                        lo = 2 * sh - 1
                        P.add("dve", lambda e, dst=dst, src=src, lo=lo, sh=sh: e.tensor_tensor(
                            out=dst[:, :, lo:528], in0=src[:, :, lo:528], in1=src[:, :, lo - sh:528 - sh], op=ALU.add),
                            reads=skeys, writes=dkeys)
                        src, skeys = dst, dkeys
                        sh *= 2
                    w = 2 ** (g + 1)
                    P.add("dve", lambda e, src=src, w=w, dg=dg: e.scalar_tensor_tensor(
                        out=dg[:, :, :], in0=src[:, :, 16:528], scalar=1.0 / w, in1=uG[:, :, 16:528],
                        op0=ALU.mult, op1=ALU.subtract),
                        reads=skeys + UK, writes=DK)
                    if tt == 0:
                        for c2 in range(2):
                            P.add("dve", lambda e, src=src, w=w, c2=c2: e.tensor_tensor(
                                out=src[:, c2, 16:16 + w - 1], in0=src[:, c2, 16:16 + w - 1], in1=invc[:, 0:w - 1],
                                op=ALU.mult), reads=skeys + ["invc"], writes=skeys)
                            P.add("dve", lambda e, src=src, w=w, c2=c2, dg=dg: e.tensor_tensor(
                                out=dg[:, c2, 0:w - 1], in0=src[:, c2, 16:16 + w - 1], in1=uG[:, c2, 16:16 + w - 1],
                                op=ALU.subtract), reads=skeys + UK, writes=DK)

                def pool_mm(g):
                    dg, DK = dbuf(g)
                    for oc in range(2):
                        b = nb()
                        for k2 in range(2):
                            P.add("pe", lambda e, b=b, k2=k2, oc=oc, g=g, dg=dg: e.matmul(
                                psum[:, b, :], pwv[:, g * 2 + k2, oc * 128:(oc + 1) * 128], dg[:, k2, :],
                                start=(k2 == 0), stop=(k2 == 1)),
                                reads=["wpool"] + DK, writes=[("ps", b)])
                        tick()
                        yc = 8 + 2 * g + oc
                        evac_copy(yT[:, yc, :], psum[:, b, :], [("ps", b)], pg(8 + yc, 1),
                                  scale=cst[:, PSC + 2 * g + oc:PSC + 2 * g + oc + 1])

                u_block(0)
                q_blocks()
                u_block(1)
                pool_mm(0)
                k_blocks()
                u_block(2)
                pool_mm(1)
                make_rT()
                v_blocks(8, 10)
                u_block(3)
                pool_mm(2)
                v_blocks(10, 12)
                pool_mm(3)
                PTC = pg(24, 5)
                YK = pg(30, 4)
                iters = [(qb, h) for qb in range(4) for h in range(16)]
                NIT = len(iters)

                def kslice(qb, j):
                    kb = qb - 4 + j
                    if kb >= 0:
                        return cur, kb
                    return prv, kb + 4

                def jmin_of(qb):
                    return max(0, 4 - qb) if tt == 0 else 0

                def S(it):
                    qb, h = iters[it]
                    ch, pb = h // 2, (h % 2) * 64
                    bm, bl = 2 * (it % 2), 2 * (it % 2) + 1
                    for j in range(jmin_of(qb), 5):
                        hf, kb = kslice(qb, j)
                        rd = []
                        if j < 4:
                            o_ap, wr = psum[:, bm, j * 128:(j + 1) * 128], [("ps", bm)]
                        else:
                            o_ap, wr = psum[:, bl, 0:128], [("ps", bl)]
                        P.add("pe", lambda e, o_ap=o_ap, hf=hf, kb=kb, ch=ch, pb=pb, qb=qb: e.matmul(
                            o_ap, KT[pb:pb + 64, ch, hf, kb * 128:(kb + 1) * 128],
                            QT[pb:pb + 64, ch, qb * 128:(qb + 1) * 128], start=True, stop=True),
                            reads=[("KT", ch, hf)] + pg(ch, 1) + rd, writes=wr)

                def X(it):
                    qb, h = iters[it]
                    bm, bl, i = 2 * (it % 2), 2 * (it % 2) + 1, it % 4
                    pt = PT[i]
                    jm = jmin_of(qb)
                    kf, k3, k4 = ("pt", i, "f"), ("pt", i, "3"), ("pt", i, "4")
                    if jm < 4:
                        P.add("act", lambda e, pt=pt, bm=bm, jm=jm: e.activation(
                            out=pt[:, jm * 128:512], in_=psum[:, bm, jm * 128:512], func=AF.Exp, scale=0.125),
                            reads=[("ps", bm)] + PTC, writes=([kf] if jm < 3 else []) + [k3])
                    P.add("act", lambda e, pt=pt, bl=bl: e.activation(
                        out=pt[:, 512:640], in_=psum[:, bl, 0:128], func=AF.Exp, scale=0.125),
                        reads=[("ps", bl)] + PTC, writes=[k4])
                    if jm == 0:
                        P.add("dve", lambda e, pt=pt: e.memset(pt[0:64, 64:128], 0.0), reads=PTC, writes=[kf])
                    lo = max(jm, 3)
                    P.add("dve", lambda e, pt=pt, h=h, lo=lo: e.tensor_tensor(
                        out=pt[:, lo * 128:640], in0=pt[:, lo * 128:640], in1=E[:, h, (lo - 3) * 128:256], op=ALU.mult),
                        reads=[("E", h // 4)] + PTC, writes=([k3] if lo == 3 else []) + [k4])

                def PV(it):
                    qb, h = iters[it]
                    i = it % 4
                    pt = PT[i]
                    jm = jmin_of(qb)
                    bo, oo = 4 + h // 7, (h % 7) * 65
                    for j in range(jm, 5):
                        hf, kb = kslice(qb, j)
                        kk = ("pt", i, "f") if j < 3 else ("pt", i, str(j))
                        P.add("pe", lambda e, bo=bo, oo=oo, pt=pt, j=j, hf=hf, kb=kb, h=h, jm=jm: e.matmul(
                            psum[:, bo, oo:oo + 65], pt[:, j * 128:(j + 1) * 128], V[:, hf * 4 + kb, h, :],
                            start=(j == jm), stop=(j == 4)),
                            reads=[kk, ("V", hf * 4 + kb)] + PTC, writes=[("ps", bo)])

                def NORM(bo, h0, nh):
                    ov = psum[:, bo, 0:nh * 65].rearrange("p (h d) -> p h d", d=65)
                    P.add("dve", lambda e, ov=ov, h0=h0, nh=nh: e.reciprocal(
                        rcp[:, h0:h0 + nh].unsqueeze(2), ov[:, :, 64:65]),
                        reads=[("ps", bo)], writes=[("rcp", bo)])
                    for hh in range(nh):
                        h = h0 + hh
                        P.add("dve", lambda e, ov=ov, hh=hh, h=h: e.tensor_scalar(
                            out=ytok[:, h * 64:(h + 1) * 64], in0=ov[:, hh, 0:64], scalar1=rcp[:, h:h + 1],
                            scalar2=None, op0=ALU.mult),
                            reads=[("ps", bo), ("rcp", bo)] + YK, writes=[("yt", h)])

                def TRp(qb, c4):
                    for ci in range(4):
                        c = c4 * 4 + ci
                        P.add("pe", lambda e, ci=ci, c=c: e.transpose(
                            psum[:, 7, ci * 128:(ci + 1) * 128], ytok[:, c * 128:(c + 1) * 128], ident[:]),
                            reads=YK + ["ident", ("yt", 2 * c), ("yt", 2 * c + 1)], writes=[("ps", 7)])

                def TRe(qb, c4):
                    evac_copy(yT[:, c4 * 4:(c4 + 1) * 4, qb * 128:(qb + 1) * 128],
                              psum[:, 7, :].rearrange("p (c t) -> p c t", t=128),
                              [("ps", 7)], pg(8 + c4 * 4, 4))

                for it in range(min(2, NIT)):
                    S(it)
                late = []
                for it in range(NIT):
                    qb, h = iters[it]
                    X(it)
                    for fn in late:
                        fn()
                    late[:] = []
                    PV(it)
                    if it + 2 < NIT:
                        S(it + 2)
                    if h == 6:
                        late.append(lambda: NORM(4, 0, 7))
                    elif h == 13:
                        late.append(lambda: NORM(5, 7, 7))
                    elif h == 15:
                        late.append(lambda: NORM(6, 14, 2))
                    if qb > 0 and h in (2, 4):
                        c4 = 0 if h == 2 else 1
                        TRp(qb - 1, c4)
                        late.append(lambda qb=qb, c4=c4: TRe(qb - 1, c4))
                for fn in late:
                    fn()
                late[:] = []
                TRp(3, 0)
                TRe(3, 0)
                TRp(3, 1)
                TRe(3, 1)
                proj_fm("w_out", 8, lambda kc: yT[:, kc, :], lambda kc: pg(8 + kc, 1), KC,
                        lambda dc, b: resid_add(dc, b, xn_for="cross"))

            def cross(ti):
                qblk = [ws.get("w_cq", 0), ws.get("w_cq", 1)]
                qbank = [nb() for _ in range(4)]
                for kc in range(KC):
                    for c in range(4):
                        sl, wv = qblk[c // 2]
                        P.add("pe", lambda e, b=qbank[c], wv=wv, kc=kc, c2=c % 2: e.matmul(
                            psum[:, b, :], wv[:, kc, c2 * 128:(c2 + 1) * 128], xnT[:, kc, :],
                            start=(kc == 0), stop=(kc == KC - 1)),
                            reads=[("ws", sl), ("xn", kc)], writes=[("ps", qbank[c])])
                    if kc == 3:
                        norm_stats("cross")
                tick()
                for c in range(4):
                    P.add("dve", lambda e, c=c: e.tensor_tensor(
                        out=QcT[:, c, :], in0=psum[:, qbank[c], :], in1=rstd[:], op=ALU.mult),
                        reads=[("ps", qbank[c]), "rstd"], writes=pg(c, 1))
                sc = float(128 ** -0.5)

                def CS(h):
                    i = h % 2
                    pc = PTc[i]
                    for mb in range(2):
                        b = nb()
                        P.add("pe", lambda e, b=b, h=h, mb=mb: e.matmul(
                            psum[:, b, :], KcT[:, h, mb * 128:(mb + 1) * 128], QcT[:, h, :], start=True, stop=True),
                            reads=["KcT"] + pg(h, 1), writes=[("ps", b)])
                        P.add("act", lambda e, b=b, pc=pc, mb=mb: e.activation(
                            out=pc[:, mb, :], in_=psum[:, b, :], func=AF.Exp, scale=sc),
                            reads=[("ps", b)], writes=pg(4 + 2 * i + mb, 1))

                def CV(h):
                    i = h % 2
                    pc, pck = PTc[i], pg(4 + 2 * i, 2)
                    bd, bv = nb(), nb()
                    for mb in range(2):
                        P.add("pe", lambda e, bd=bd, pc=pc, mb=mb: e.matmul(
                            psum[:, bd, :], ones[:], pc[:, mb, :], start=(mb == 0), stop=(mb == 1)),
                            reads=pck + ["ones"], writes=[("ps", bd)])
                    for mb in range(2):
                        P.add("pe", lambda e, bv=bv, pc=pc, mb=mb, h=h: e.matmul(
                            psum[:, bv, :], Vc[:, mb, h * 128:(h + 1) * 128], pc[:, mb, :],
                            start=(mb == 0), stop=(mb == 1)),
                            reads=pck + ["Vc"], writes=[("ps", bv)])
                    P.add("dve", lambda e, bd=bd: e.reciprocal(rden, psum[:, bd, :]),
                          reads=[("ps", bd)], writes=pg(8, 2))
                    P.add("dve", lambda e, bv=bv, h=h: e.tensor_tensor(
                        out=ocT[:, h, :], in0=psum[:, bv, :], in1=rden, op=ALU.mult),
                        reads=[("ps", bv)] + pg(8, 2), writes=pg(10 + h, 1))

                CS(0)
                CS(1)
                for h in range(4):
                    CV(h)
                    if h + 2 < 4:
                        CS(h + 2)
                for bi in range(2):
                    sl, wv = ws.get("w_co", bi)
                    c0 = 0
                    if bi == 0:
                        cb = [nb() for _ in range(4)]
                        for kc in range(4):
                            for c in range(4):
                                P.add("pe", lambda e, b=cb[c], wv=wv, kc=kc, c=c: e.matmul(
                                    psum[:, b, :], wv[:, kc, c * 128:(c + 1) * 128], ocT[:, kc, :],
                                    start=(kc == 0), stop=(kc == 3)),
                                    reads=[("ws", sl)] + pg(10 + kc, 1), writes=[("ps", cb[c])])
                        tick()
                        for c in range(4):
                            resid_add(c, cb[c], xn_for="ffn2")
                        c0 = 4
                    for c2 in range(c0, 8):
                        b = nb()
                        for kc in range(4):
                            P.add("pe", lambda e, b=b, wv=wv, kc=kc, c2=c2: e.matmul(
                                psum[:, b, :], wv[:, kc, c2 * 128:(c2 + 1) * 128], ocT[:, kc, :],
                                start=(kc == 0), stop=(kc == 3)),
                                reads=[("ws", sl)] + pg(10 + kc, 1), writes=[("ps", b)])
                        tick()
                        resid_add(bi * 8 + c2, b, xn_for="ffn2")

            def final(ti, nxt):
                for tb in range(4):
                    ost = ar_f32(8 * tb, 8)
                    obank = [nb() for _ in range(4)]
                    for c4 in range(4):
                        for ci in range(4):
                            c = c4 * 4 + ci
                            P.add("pe", lambda e, b=obank[c4], ci=ci, c=c, tb=tb: e.transpose(
                                psum[:, b, ci * 128:(ci + 1) * 128], hT[:, c, tb * 128:(tb + 1) * 128], ident[:]),
                                reads=[("hT", c, tb), "ident"], writes=[("ps", obank[c4])])
                        tick()
                    early_x = (tb == 0 and nxt)
                    if tb == 0:
                        norm_stats("final")
                        hold.update(obank)
                        if early_x:
                            xpose_x(ti + 1, tb)
                        make_rT()
                        hold.clear()
                    for c4 in range(4):
                        evac_copy(ost[:, c4 * 512:(c4 + 1) * 512], psum[:, obank[c4], :],
                                  [("ps", obank[c4]), "rT"], pg(8 * tb + 2 * c4, 2), scale=rT[:, tb:tb + 1])
                    r0 = ti * T + tb * 128
                    P.add("pool", lambda e, ost=ost, r0=r0: e.dma_start(out=out_d[r0:r0 + 128, :], in_=ost),
                          reads=pg(8 * tb, 8), lane="o%d" % (tb % 2))
                    if nxt:
                        if not early_x:
                            xpose_x(ti + 1, tb)
                        if tb + 2 < 4:
                            issue_x(ti + 1, tb + 2)

            issue_x(0, 0)
            issue_x(0, 1)
            xpose_x(0, 0)
            issue_x(0, 2)
            xpose_x(0, 1)
            issue_x(0, 3)
            seq_loads(0)
            ws.emit_casts()
            P.add("sp", lambda e: e.dma_start(out=wpool_sb[:].rearrange("p k n -> p (k n)"), in_=scr["w_pool"][0]),
                  reads=[("scr", "w_pool", 0)], writes=["wpool"], lane="wp")
            xpose_x(0, 2)
            xpose_x(0, 3)
            for ti in range(ntiles):
                if ti % TPS == 0:
                    if ti > 0:
                        seq_loads(ti // TPS)
                    seq_start(ti // TPS)
                ffn("ffn1", "mix")
                mixer(ti)
                cross(ti)
                nxt = ti + 1 < ntiles
                if nxt:
                    issue_x(ti + 1, 0)
                    issue_x(ti + 1, 1)
                ffn("ffn2", "final")
                final(ti, nxt)

        class WS:
            def __init__(self, P, plan=None):
                self.P = P
                self.plan = plan
                self.rec = []
                self.i = 0
                self.loaded = 0
                self.seen = set()

            def view(self, name, slot):
                _, kc, _, nbw = wblocks[name][0]
                return wsl[slot][:, 0:kc * nbw].rearrange("p (k n) -> p k n", n=nbw)

            def emit_casts(self):
                if self.plan is None:
                    return
                todo = [("w_pool", 0)]
                if not DIRECT:
                    for nb_ in self.plan:
                        if nb_ not in todo:
                            todo.append(nb_)
                for n, (name, bi) in enumerate(todo):
                    self.seen.add((name, bi))
                    r0, kc, c0, nbw = wblocks[name][bi]
                    src = W[name][r0:r0 + kc * 128, c0:c0 + nbw].rearrange("(k p) n -> p k n", p=128)
                    dst = scr[name][bi].rearrange("p (k n) -> p k n", n=nbw)
                    self.P.add("pool", lambda e, src=src, dst=dst: e.dma_start(out=dst, in_=src),
                               writes=[("scr", name, bi)], lane="c%d" % (n % 8))

            def _load(self, k):
                name, bi = self.plan[k]
                slot = k % NSLOT
                r0, kc, c0, nbw = wblocks[name][bi]
                if (name, bi) not in self.seen:
                    self.seen.add((name, bi))
                    src = W[name][r0:r0 + kc * 128, c0:c0 + nbw].rearrange("(k p) n -> p k n", p=128)
                    dstv = wsl[slot][:, 0:kc * nbw].rearrange("p (k n) -> p k n", n=nbw)
                    self.P.add("pool", lambda e, src=src, dstv=dstv: e.dma_start(out=dstv, in_=src),
                               writes=[("ws", slot)], lane="v%d" % slot)
                    self.P.add("sp", lambda e, name=name, bi=bi, slot=slot, kc=kc, nbw=nbw: e.dma_start(
                        out=scr[name][bi], in_=wsl[slot][:, 0:kc * nbw]),
                        reads=[("ws", slot)], writes=[("scr", name, bi)], lane="s%d" % slot)
                else:
                    self.P.add("sp", lambda e, name=name, bi=bi, slot=slot, kc=kc, nbw=nbw: e.dma_start(
                        out=wsl[slot][:, 0:kc * nbw], in_=scr[name][bi]),
                        reads=[("scr", name, bi)], writes=[("ws", slot)], lane="w%d" % slot)

            def get(self, name, bi):
                k = self.i
                self.i += 1
                if self.plan is None:
                    self.rec.append((name, bi))
                    return 0, self.view(name, 0)
                assert self.plan[k] == (name, bi)
                while self.loaded < min(k + NSLOT - 2, len(self.plan) - 1) + 1:
                    self._load(self.loaded)
                    self.loaded += 1
                return k % NSLOT, self.view(name, k % NSLOT)

        dryP = Prog(dry=True)
        dws = WS(dryP)
        emit_all(dryP, dws)
        P = Prog()
        ws = WS(P, plan=dws.rec)
        emit_all(P, ws)
        P.emit(nc, ctx)
    return nc


def _host_tables(rel_bias, norms, pool_scale):
    cst = np.zeros((128, 128), np.float32)
    for i, g in enumerate(norms):
        cst[:, 16 * i:16 * i + 16] = np.asarray(g, np.float32).reshape(16, 128).T
    cst[:, 96:104] = np.asarray(pool_scale, np.float32).reshape(8, 128).T
    rb = np.asarray(rel_bias, np.float32).reshape(16, 257)
    cst[:, 104:120] = rb[:, 256][None, :]
    cst[:, 127] = EPS
    k = np.arange(128)[:, None]
    q = np.arange(128)[None, :]
    idx3 = np.clip(128 + q - k, -128, 128) + 128
    idx4 = np.clip(q - k, -128, 128) + 128
    bm = np.stack([rb[:, idx3], rb[:, idx4]], axis=2)
    bm = np.ascontiguousarray(bm.transpose(1, 0, 2, 3)).reshape(128, 16 * 256)
    return cst, bm


_NC_CACHE = {}


def kernel(x, mem, ffn1_norm, ffn1_w_gate, ffn1_w_up, ffn1_w_down, mix_norm, w_in, rel_bias, w_pool,
           pool_scale, w_out, cross_norm, mem_norm, w_cq, w_ckv, w_co, ffn2_norm, ffn2_w_gate,
           ffn2_w_up, ffn2_w_down, final_norm):
    f = lambda a: np.ascontiguousarray(np.asarray(a, dtype=np.float32))
    x = f(x)
    mem = f(mem)
    cst, bm = _host_tables(rel_bias, [f(ffn1_norm)[0], f(mix_norm)[0], f(cross_norm)[0], f(ffn2_norm)[0],
                                      f(final_norm), f(mem_norm)[0]], f(pool_scale)[0])
    shared = {
        "cst": cst, "bmat": bm,
        "ffn1_g": f(ffn1_w_gate)[0], "ffn1_u": f(ffn1_w_up)[0], "ffn1_d": f(ffn1_w_down)[0],
        "ffn2_g": f(ffn2_w_gate)[0], "ffn2_u": f(ffn2_w_up)[0], "ffn2_d": f(ffn2_w_down)[0],
        "w_in": f(w_in)[0], "w_pool": f(w_pool)[0].reshape(1024, 256), "w_out": f(w_out)[0],
        "w_cq": f(w_cq)[0], "w_ckv": f(w_ckv)[0], "w_co": f(w_co)[0],
    }
    if "nc" not in _NC_CACHE:
        _NC_CACHE["nc"] = build_program()
    nc = _NC_CACHE["nc"]
    in_maps = []
    for c in range(NCORES):
        m = dict(shared)
        m["x"] = x[2 * c:2 * c + 2].reshape(NSEQ * SEQ, D)
        m["mem"] = mem[2 * c:2 * c + 2].reshape(NSEQ * NMEM, D)
        in_maps.append(m)
    res = run_bass_kernel_spmd(nc, in_maps, core_ids=list(range(NCORES)))
    out = np.stack([np.asarray(r["out"], dtype=np.float32).reshape(NSEQ, SEQ, D) for r in res.results], axis=0)
    return out.reshape(16, SEQ, D)
```

```python
import contextlib
import numpy as np
import concourse.bass as bass
import concourse.mybir as mybir
from concourse.bass_utils import run_bass_kernel_spmd

F32 = mybir.dt.float32
BF16 = mybir.dt.bfloat16
I32 = mybir.dt.int32
AF = mybir.ActivationFunctionType
ALU = mybir.AluOpType

NCORES = 8
D = 2048
DFF = 5632
T = 512
SEQ = 2048
TPS = SEQ // T
NSEQ = 2
NMEM = 256
KC = D // 128
EPS = 1e-6
NSLOT = 6
SLOTW = 4096
NPAGE = 34
import os
PRUNE = os.environ.get('K_PRUNE', '1') == '1'
DIRECT = os.environ.get('K_DIRECT', '0') == '1'


class Op:
    __slots__ = ("eng", "fn", "deps", "lane", "signal", "semval")

    def __init__(self, eng, fn, deps, lane):
        self.eng, self.fn, self.deps, self.lane = eng, fn, deps, lane
        self.signal = False
        self.semval = None


class Prog:
    ENGS = ("pe", "act", "dve", "pool", "sp")

    def __init__(self, dry=False):
        self.ops = []
        self.lastw = {}
        self.readers = {}
        self.dry = dry

    def add(self, eng, fn, reads=(), writes=(), lane=None):
        if self.dry:
            return
        i = len(self.ops)
        writes = list(writes)
        if lane is not None:
            writes.append(("lane", lane))
        deps = set()
        for r in reads:
            w = self.lastw.get(r)
            if w is not None:
                deps.add(w)
        for r in writes:
            w = self.lastw.get(r)
            if w is not None:
                deps.add(w)
            rs = self.readers.get(r)
            if rs:
                deps.update(rs)
        for r in reads:
            self.readers.setdefault(r, []).append(i)
        for r in writes:
            self.lastw[r] = i
            self.readers[r] = []
        deps.discard(i)
        self.ops.append(Op(eng, fn, deps, lane))

    def emit(self, nc, ctx):
        ops = self.ops
        for o in ops:
            if o.eng == "pe":
                o.deps = {d for d in o.deps if ops[d].eng != "pe"}
            if not PRUNE:
                continue
            best = {}
            keep = set()
            for d in o.deps:
                pe_ = ops[d]
                if pe_.lane is not None:
                    keep.add(d)
                elif best.get(pe_.eng, -1) < d:
                    best[pe_.eng] = d
            keep.update(best.values())
            o.deps = keep
        for o in ops:
            for d in o.deps:
                ops[d].signal = True
        lanes = sorted({o.lane for o in ops if o.lane is not None})
        sems = {}
        for e in ("pe", "act", "dve", "pool"):
            sems[e] = ctx.enter_context(nc.semaphore("s_" + e))
        for l in lanes:
            sems[("lane", l)] = ctx.enter_context(nc.semaphore("l_%s" % (l,)))
        cnt = {k: 0 for k in sems}
        for o in ops:
            if o.lane is not None:
                k = ("lane", o.lane)
                cnt[k] += 16
                o.semval = (k, cnt[k])
            elif o.signal:
                cnt[o.eng] += 1
                o.semval = (o.eng, cnt[o.eng])
        streams = {e: [o for o in ops if o.eng == e] for e in self.ENGS}
        block = ctx.enter_context(nc.Block())

        def run(stream, e):
            waited = {}
            for o in stream:
                need = {}
                for d in o.deps:
                    k, v = ops[d].semval
                    if need.get(k, 0) < v:
                        need[k] = v
                for k, v in need.items():
                    if waited.get(k, 0) < v:
                        e.wait_ge(sems[k], v)
                        waited[k] = v
                ins = o.fn(e)
                if o.semval is not None:
                    ins.then_inc(sems[o.semval[0]], 16 if o.lane is not None else 1)

        @block.tensor
        def _(e):
            run(streams["pe"], e)

        @block.scalar
        def _(e):
            run(streams["act"], e)

        @block.vector
        def _(e):
            run(streams["dve"], e)

        @block.gpsimd
        def _(e):
            run(streams["pool"], e)
            for l in lanes:
                if str(l).startswith("o"):
                    e.wait_ge(sems[("lane", l)], cnt[("lane", l)])

        @block.sync
        def _(e):
            run(streams["sp"], e)


def build_program(ntiles=NSEQ * TPS):
    nc = bass.Bass("TRN2", target_bir_lowering=False)
    ntok = ntiles * T
    dr = lambda name, shape, dt=F32, kind="ExternalInput": nc.dram_tensor(name, shape, dt, kind=kind).ap()
    x_d = dr("x", [NSEQ * SEQ, D])
    mem_d = dr("mem", [NSEQ * NMEM, D])
    out_d = dr("out", [NSEQ * SEQ, D], kind="ExternalOutput")
    cst_d = dr("cst", [128, 128])
    bm_d = dr("bmat", [128, 16 * 256])
    W = {}
    for f in ("ffn1", "ffn2"):
        W[f + "_g"] = dr(f + "_g", [D, DFF])
        W[f + "_u"] = dr(f + "_u", [D, DFF])
        W[f + "_d"] = dr(f + "_d", [DFF, D])
    W["w_in"] = dr("w_in", [D, 4096])
    W["w_pool"] = dr("w_pool", [1024, 256])
    W["w_out"] = dr("w_out", [D, D])
    W["w_cq"] = dr("w_cq", [D, 512])
    W["w_ckv"] = dr("w_ckv", [D, 1024])
    W["w_co"] = dr("w_co", [512, D])

    wblocks = {}
    for f in ("ffn1", "ffn2"):
        wblocks[f + "_g"] = [(0, 16, j * 256, 256) for j in range(22)]
        wblocks[f + "_u"] = [(0, 16, j * 256, 256) for j in range(22)]
        wblocks[f + "_d"] = [((hh * 22 + rg * 11) * 128, 11, cb * 256, 256)
                             for hh in range(2) for rg in range(2) for cb in range(8)]
    wblocks["w_in"] = [(0, 16, j * 256, 256) for j in range(16)]
    wblocks["w_pool"] = [(0, 8, 0, 256)]
    wblocks["w_out"] = [(0, 16, j * 256, 256) for j in range(8)]
    wblocks["w_cq"] = [(0, 16, j * 256, 256) for j in range(2)]
    wblocks["w_ckv"] = [(0, 16, j * 256, 256) for j in range(4)]
    wblocks["w_co"] = [(0, 4, j * 1024, 1024) for j in range(2)]
    scr = {}
    for name, bl in wblocks.items():
        scr[name] = nc.dram_tensor("scr_" + name, [len(bl), 128, bl[0][1] * bl[0][3]], BF16).ap()

    with contextlib.ExitStack() as ctx:
        sb = lambda name, shape, dt: ctx.enter_context(nc.sbuf_tensor(name, shape, dt))
        hT = sb("hT", [128, KC, T], F32)
        xnT = sb("xnT", [128, KC, T], BF16)
        arena = sb("arena", [128, NPAGE * 512], BF16)
        KT = sb("KT", [128, 8, 2, T], BF16)
        V = sb("V", [128, 8, 16, 65], BF16)
        E = sb("E", [128, 16, 256], BF16)
        xst = [sb("xst%d" % i, [128, D], F32) for i in range(2)]
        wsl = [sb("wsl%d" % i, [128, SLOTW], BF16) for i in range(NSLOT)]
        KcT = sb("KcT", [128, 4, NMEM], BF16)
        Vc = sb("Vc", [128, 2, 512], BF16)
        sq = [sb("sq%d" % i, [128, T], BF16) for i in range(3)]
        dG1 = sb("dG1", [128, 2, T], BF16)
        rstd = sb("rstd", [128, T], F32)
        sg = [sb("sg%d" % i, [128, T], F32) for i in range(2)]
        cst = sb("cst_sb", [128, 128], F32)
        ident = sb("ident", [128, 128], F32)
        ones = sb("ones", [128, 128], BF16)
        halo = sb("halo", [128, 8, 16], F32)
        invc = sb("invc", [128, 16], F32)
        invc_i = sb("invc_i", [128, 16], I32)
        rcp = sb("rcp", [128, 16], F32)
        mst = sb("mst", [128, 4], F32)
        nbf = sb("nbf", [128, 16], F32)
        rT = sb("rT", [128, 4], F32)
        wpool_sb = sb("wpool_sb", [128, 8, 256], BF16)
        psum = ctx.enter_context(nc.psum_tensor("psum", [128, 8, 512], F32))

        GC = {"ffn1": 0, "mix": 16, "cross": 32, "ffn2": 48, "final": 64, "mem": 80}
        PSC = 96
        BFAR = 104

        def pg(p0, npg):
            return [("ar", p) for p in range(p0, p0 + npg)]

        def ar_bf(p0, npages):
            return arena[:, p0 * 512:(p0 + npages) * 512]

        def ar_f32(p0, npages):
            return arena[:, p0 * 512:(p0 + npages) * 512].bitcast(F32)

        hid = ar_bf(0, 22).rearrange("p (c t) -> p c t", t=T)
        QT = ar_bf(0, 8).rearrange("p (c t) -> p c t", t=T)
        yT = ar_bf(8, 16).rearrange("p (c t) -> p c t", t=T)
        uG = ar_f32(24, 5)[:, 0:1056].rearrange("p (c t) -> p c t", t=528)
        tA = ar_f32(29, 5)[:, 0:1056].rearrange("p (c t) -> p c t", t=528)
        tB = ar_f32(8, 5)[:, 0:1056].rearrange("p (c t) -> p c t", t=528)
        dG = ar_bf(13, 2).rearrange("p (c t) -> p c t", t=T)
        PT = [arena[:, 24 * 512 + i * 640:24 * 512 + (i + 1) * 640] for i in range(4)]
        ytok = ar_f32(30, 4)
        QcT = ar_bf(0, 4).rearrange("p (c t) -> p c t", t=T)
        PTc = [ar_bf(4 + 2 * i, 2).rearrange("p (c t) -> p c t", t=T) for i in range(2)]
        rden = ar_f32(8, 2)
        ocT = ar_bf(10, 4).rearrange("p (c t) -> p c t", t=T)
        memst = [ar_f32(8 * i, 8) for i in range(2)]
        mnT = ar_bf(16, 8).rearrange("p (c t) -> p c t", t=NMEM)
        bmst = ar_f32(0, 16)

        hk = lambda c: [("hT", c, tb) for tb in range(4)]

        def emit_all(P, ws):
            bank_ctr = [0]

            hold = set()

            def nb():
                while True:
                    b = bank_ctr[0] % 7
                    bank_ctr[0] += 1
                    if b not in hold:
                        return b

            SB = 7
            pend = []
            stat_n = [0]

            def tick():
                keep = []
                for ent in pend:
                    if ent[0] >= 1:
                        ent[1]()
                    else:
                        ent[0] += 1
                        keep.append(ent)
                pend[:] = keep

            def flush():
                for ent in pend:
                    ent[1]()
                pend[:] = []

            def ensure_free(tag):
                last = -1
                for i, ent in enumerate(pend):
                    if ent[2] == tag:
                        last = i
                for ent in pend[:last + 1]:
                    ent[1]()
                pend[:] = pend[last + 1:]

            def stat_hook(dc):
                k = stat_n[0]
                stat_n[0] += 1
                sbuf = sq[k % 3]
                ensure_free(k % 3)
                P.add("act", lambda e, dc=dc, sbuf=sbuf: e.activation(out=sbuf[:], in_=hT[:, dc, :], func=AF.Square),
                      reads=hk(dc), writes=[("sq", k % 3)])

                def mm(k=k, sbuf=sbuf):
                    P.add("pe", lambda e: e.matmul(psum[:, SB, :], ones[:], sbuf[:], start=(k == 0), stop=(k == KC - 1)),
                          reads=[("sq", k % 3), "ones"], writes=[("ps", SB)])
                pend.append([0, mm, k % 3])

            evq = [0]

            P.add("pool", lambda e: e.dma_start(out=cst[:], in_=cst_d), writes=["cst"], lane="x0")
            P.add("pool", lambda e: e.dma_start(out=bmst, in_=bm_d), writes=pg(0, 16), lane="x1")
            P.add("pool", lambda e: e.memset(ident[:], 0.0), writes=["ident"])
            P.add("pool", lambda e: e.affine_select(out=ident[:], in_=ident[:], pattern=[[-1, 128]],
                                                    compare_op=ALU.not_equal, fill=1.0, base=0,
                                                    channel_multiplier=1),
                  reads=["ident"], writes=["ident"])
            P.add("pool", lambda e: e.memset(ones[:], 1.0), writes=["ones"])
            P.add("pool", lambda e: e.memset(V[:], 1.0), writes=[("V", i) for i in range(8)])
            P.add("pool", lambda e: e.iota(invc_i[:], [[1, 16]], base=1, channel_multiplier=0),
                  writes=["invc_i"])
            P.add("dve", lambda e: e.tensor_copy(invc[:], invc_i[:]), reads=["invc_i"], writes=["invc"])
            P.add("dve", lambda e: e.reciprocal(invc[:], invc[:]), reads=["invc"], writes=["invc"])
            P.add("act", lambda e: e.activation(out=nbf[:], in_=cst[:, BFAR:BFAR + 16], func=AF.Copy, scale=-1.0),
                  reads=["cst"], writes=["nbf"])
            for h in range(16):
                P.add("act", lambda e, h=h: e.activation(
                    out=E[:, h, :], in_=bmst[:, h * 256:(h + 1) * 256], func=AF.Exp, bias=nbf[:, h:h + 1]),
                    reads=pg(h, 1) + ["nbf"], writes=[("E", h // 4)])
            P.add("dve", lambda e: e.memset(E[64:128, :, 128:192], 0.0),
                  reads=[("E", i) for i in range(4)], writes=[("E", i) for i in range(4)])

            def evac_copy(out, in_, reads, writes, scale=None):
                evq[0] += 1
                if scale is not None:
                    P.add("act", lambda e: e.activation(out=out, in_=in_, func=AF.Copy, scale=scale),
                          reads=reads, writes=writes)
                elif evq[0] % 2:
                    P.add("act", lambda e: e.activation(out=out, in_=in_, func=AF.Copy),
                          reads=reads, writes=writes)
                else:
                    P.add("dve", lambda e: e.tensor_copy(out, in_), reads=reads, writes=writes)

            def norm_stats(gname):
                flush()
                assert stat_n[0] == KC, stat_n[0]
                stat_n[0] = 0
                P.add("act", lambda e: e.activation(out=rstd[:], in_=psum[:, SB, :], func=AF.Sqrt,
                                                    scale=1.0 / D, bias=cst[:, 127:128]),
                      reads=[("ps", SB), "cst"], writes=["rstd"])
                P.add("dve", lambda e: e.reciprocal(rstd[:], rstd[:]), reads=["rstd"], writes=["rstd"])

            def norm(gname):
                norm_stats(gname)
                g0 = GC[gname]
                for c in range(KC):
                    P.add("dve", lambda e, c=c: e.scalar_tensor_tensor(
                        out=xnT[:, c, :], in0=hT[:, c, :], scalar=cst[:, g0 + c:g0 + c + 1], in1=rstd[:],
                        op0=ALU.mult, op1=ALU.mult),
                        reads=hk(c) + ["rstd", "cst"], writes=[("xn", c)])

            XN = [("xn", c) for c in range(KC)]

            def ffn(f, nxt):
                ci = 0
                for hh in range(2):
                    for fbl in range(11):
                        fb = hh * 11 + fbl
                        gs, gv = ws.get(f + "_g", fb)
                        us, uv = ws.get(f + "_u", fb)
                        first = (hh == 0 and fbl == 0)
                        banks = [(nb(), nb()) for _ in range(2)]
                        if first:
                            for kc in range(KC):
                                for c2 in range(2):
                                    for (bk, wv, sl) in ((banks[c2][0], gv, gs), (banks[c2][1], uv, us)):
                                        P.add("pe", lambda e, bk=bk, wv=wv, kc=kc, c2=c2: e.matmul(
                                            psum[:, bk, :], wv[:, kc, c2 * 128:(c2 + 1) * 128], xnT[:, kc, :],
                                            start=(kc == 0), stop=(kc == KC - 1)),
                                            reads=[("ws", sl), ("xn", kc)], writes=[("ps", bk)])
                                if kc == 3:
                                    norm_stats(f)
                        for c2 in range(2):
                            lc = fbl * 2 + c2
                            bg, bu = banks[c2]
                            if not first:
                                for (bk, wv, sl) in ((bg, gv, gs), (bu, uv, us)):
                                    for kc in range(KC):
                                        P.add("pe", lambda e, bk=bk, wv=wv, kc=kc, c2=c2: e.matmul(
                                            psum[:, bk, :], wv[:, kc, c2 * 128:(c2 + 1) * 128], xnT[:, kc, :],
                                            start=(kc == 0), stop=(kc == KC - 1)),
                                            reads=[("ws", sl), ("xn", kc)], writes=[("ps", bk)])
                            tick()
                            s = sg[ci % 2]
                            sk = ("sg", ci % 2)
                            P.add("dve", lambda e, s=s, bg=bg: e.tensor_tensor(
                                out=s[:], in0=psum[:, bg, :], in1=rstd[:], op=ALU.mult),
                                reads=[("ps", bg), "rstd"], writes=[sk])
                            P.add("act", lambda e, s=s: e.activation(out=s[:], in_=s[:], func=AF.Silu),
                                  reads=[sk], writes=[sk])
                            P.add("dve", lambda e, s=s, bu=bu: e.tensor_tensor(
                                out=s[:], in0=s[:], in1=psum[:, bu, :], op=ALU.mult),
                                reads=[sk, ("ps", bu)], writes=[sk])
                            P.add("dve", lambda e, s=s, lc=lc: e.tensor_tensor(
                                out=hid[:, lc, :], in0=s[:], in1=rstd[:], op=ALU.mult),
                                reads=[sk, "rstd"], writes=pg(lc, 1))
                            ci += 1
                    for cb in range(8):
                        d0s, d0v = ws.get(f + "_d", (hh * 2 + 0) * 8 + cb)
                        d1s, d1v = ws.get(f + "_d", (hh * 2 + 1) * 8 + cb)
                        for d2 in range(2):
                            dc = cb * 2 + d2
                            b = nb()
                            for rg, (dv, ds) in enumerate(((d0v, d0s), (d1v, d1s))):
                                for fi in range(11):
                                    lc = rg * 11 + fi
                                    P.add("pe", lambda e, b=b, dv=dv, fi=fi, d2=d2, lc=lc, rg=rg: e.matmul(
                                        psum[:, b, :], dv[:, fi, d2 * 128:(d2 + 1) * 128], hid[:, lc, :],
                                        start=(rg == 0 and fi == 0), stop=(rg == 1 and fi == 10)),
                                        reads=[("ws", ds)] + pg(lc, 1), writes=[("ps", b)])
                            tick()
                            P.add("dve", lambda e, b=b, dc=dc: e.scalar_tensor_tensor(
                                out=hT[:, dc, :], in0=psum[:, b, :], scalar=0.5, in1=hT[:, dc, :],
                                op0=ALU.mult, op1=ALU.add),
                                reads=[("ps", b)] + hk(dc), writes=hk(dc))
                            if hh == 1:
                                stat_hook(dc)
                                if nxt == "mix":
                                    xn_plain(dc, "mix")
                                elif nxt == "final":
                                    hg_inplace(dc)

            def proj_fm(wname, nblk, rhs_of_kc, rhs_keys, nkc, consume, cols_per_blk=256):
                for bi in range(nblk):
                    s, wv = ws.get(wname, bi)
                    for c2 in range(cols_per_blk // 128):
                        b = nb()
                        for kc in range(nkc):
                            P.add("pe", lambda e, b=b, wv=wv, kc=kc, c2=c2: e.matmul(
                                psum[:, b, :], wv[:, kc, c2 * 128:(c2 + 1) * 128], rhs_of_kc(kc),
                                start=(kc == 0), stop=(kc == nkc - 1)),
                                reads=[("ws", s)] + rhs_keys(kc), writes=[("ps", b)])
                        tick()
                        consume(bi * (cols_per_blk // 128) + c2, b)

            def xn_plain(dc, gname):
                g0 = GC[gname]
                P.add("act", lambda e: e.activation(out=xnT[:, dc, :], in_=hT[:, dc, :], func=AF.Copy,
                                                    scale=cst[:, g0 + dc:g0 + dc + 1]),
                      reads=hk(dc) + ["cst"], writes=[("xn", dc)])

            def make_rT():
                b = nb()
                for tb in range(4):
                    P.add("pe", lambda e, b=b, tb=tb: e.transpose(
                        psum[:, b, tb * 128:(tb + 1) * 128], rstd[:, tb * 128:(tb + 1) * 128], ident[:]),
                        reads=["rstd", "ident"], writes=[("ps", b)])
                P.add("dve", lambda e, b=b: e.tensor_copy(
                    rT[:, :].unsqueeze(2), psum[:, b, :].rearrange("p (t c) -> p t c", c=128)[:, :, 0:1]),
                    reads=[("ps", b)], writes=["rT"])

            def hg_inplace(dc):
                g0 = GC["final"]
                P.add("act", lambda e: e.activation(out=hT[:, dc, :], in_=hT[:, dc, :], func=AF.Copy,
                                                    scale=cst[:, g0 + dc:g0 + dc + 1]),
                      reads=hk(dc) + ["cst"], writes=hk(dc))

            def resid_add(dc, b, xn_for=None):
                P.add("dve", lambda e: e.tensor_tensor(out=hT[:, dc, :], in0=psum[:, b, :], in1=hT[:, dc, :],
                                                       op=ALU.add),
                      reads=[("ps", b)] + hk(dc), writes=hk(dc))
                stat_hook(dc)
                if xn_for is not None:
                    xn_plain(dc, xn_for)

            def seq_loads(s):
                for mb in range(2):
                    ms = memst[mb]
                    r0 = s * NMEM + mb * 128
                    P.add("pool", lambda e, ms=ms, r0=r0: e.dma_start(out=ms, in_=mem_d[r0:r0 + 128, :]),
                          writes=pg(8 * mb, 8), lane="x%d" % mb)

            def seq_start(s):
                P.add("dve", lambda e: e.memset(halo[:], 0.0), writes=["halo"])
                for mb in range(2):
                    ms = memst[mb]
                    P.add("act", lambda e, ms=ms, mb=mb: e.activation(out=ar_bf(24, 4), in_=ms, func=AF.Square,
                                                                      accum_out=mst[:, mb:mb + 1]),
                          reads=pg(8 * mb, 8), writes=pg(24, 4) + [("mst", mb)])
                    P.add("act", lambda e, mb=mb: e.activation(out=mst[:, mb:mb + 1], in_=mst[:, mb:mb + 1], func=AF.Sqrt,
                                                               scale=1.0 / D, bias=cst[:, 127:128]),
                          reads=[("mst", mb), "cst"], writes=[("mst", mb)])
                    P.add("dve", lambda e, mb=mb: e.reciprocal(mst[:, mb:mb + 1], mst[:, mb:mb + 1]),
                          reads=[("mst", mb)], writes=[("mst", mb)])
                    P.add("dve", lambda e, ms=ms, mb=mb: e.tensor_scalar(out=ms, in0=ms, scalar1=mst[:, mb:mb + 1],
                                                                       scalar2=None, op0=ALU.mult),
                          reads=pg(8 * mb, 8) + [("mst", mb)], writes=pg(8 * mb, 8))
                    for c4 in range(4):
                        b = nb()
                        for ci in range(4):
                            c = c4 * 4 + ci
                            P.add("pe", lambda e, b=b, ci=ci, c=c, ms=ms: e.transpose(
                                psum[:, b, ci * 128:(ci + 1) * 128], ms[:, c * 128:(c + 1) * 128], ident[:]),
                                reads=pg(8 * mb, 8) + ["ident"], writes=[("ps", b)])
                        for ci in range(4):
                            c = c4 * 4 + ci
                            P.add("act", lambda e, b=b, ci=ci, c=c, mb=mb: e.activation(
                                out=mnT[:, c, mb * 128:(mb + 1) * 128], in_=psum[:, b, ci * 128:(ci + 1) * 128],
                                func=AF.Copy, scale=cst[:, GC["mem"] + c:GC["mem"] + c + 1]),
                                reads=[("ps", b), "cst"], writes=pg(16, 8))
                for bi in range(2):
                    sl, wv = ws.get("w_ckv", bi)
                    for c2 in range(2):
                        b = nb()
                        for kc in range(KC):
                            P.add("pe", lambda e, b=b, wv=wv, kc=kc, c2=c2: e.matmul(
                                psum[:, b, 0:NMEM], wv[:, kc, c2 * 128:(c2 + 1) * 128], mnT[:, kc, :],
                                start=(kc == 0), stop=(kc == KC - 1)),
                                reads=[("ws", sl)] + pg(16, 8), writes=[("ps", b)])
                        hh = bi * 2 + c2
                        evac_copy(KcT[:, hh, :], psum[:, b, 0:NMEM], [("ps", b)], ["KcT"])
                for bi in range(2, 4):
                    sl, wv = ws.get("w_ckv", bi)
                    for mb in range(2):
                        b = nb()
                        for kc in range(KC):
                            P.add("pe", lambda e, b=b, wv=wv, kc=kc, mb=mb: e.matmul(
                                psum[:, b, 0:256], mnT[:, kc, mb * 128:(mb + 1) * 128], wv[:, kc, :],
                                start=(kc == 0), stop=(kc == KC - 1)),
                                reads=[("ws", sl)] + pg(16, 8), writes=[("ps", b)])
                        evac_copy(Vc[:, mb, (bi - 2) * 256:(bi - 1) * 256], psum[:, b, 0:256], [("ps", b)], ["Vc"])

            def issue_x(ti, tb):
                xs = xst[tb % 2]
                r0 = ti * T + tb * 128
                P.add("pool", lambda e, xs=xs, r0=r0: e.dma_start(out=xs[:], in_=x_d[r0:r0 + 128, :]),
                      writes=[("xst", tb % 2)], lane="x%d" % (tb % 2))

            def xpose_x(ti, tb):
                xs = xst[tb % 2]
                for c4 in range(4):
                    b = nb()
                    for ci in range(4):
                        c = c4 * 4 + ci
                        P.add("pe", lambda e, b=b, ci=ci, c=c, xs=xs: e.transpose(
                            psum[:, b, ci * 128:(ci + 1) * 128], xs[:, c * 128:(c + 1) * 128], ident[:]),
                            reads=[("xst", tb % 2), "ident"], writes=[("ps", b)])
                    tick()
                    hkeys = [("hT", c4 * 4 + ci, tb) for ci in range(4)]
                    P.add("dve", lambda e, b=b, c4=c4: e.tensor_copy(
                        hT[:, c4 * 4:(c4 + 1) * 4, tb * 128:(tb + 1) * 128],
                        psum[:, b, :].rearrange("p (c t) -> p c t", t=128)),
                        reads=[("ps", b)], writes=hkeys)
                    k = c4
                    sbuf = sq[k % 3]
                    ensure_free(k % 3)
                    P.add("act", lambda e, c4=c4, sbuf=sbuf: e.activation(
                        out=sbuf[:].rearrange("p (c t) -> p c t", t=128),
                        in_=hT[:, c4 * 4:(c4 + 1) * 4, tb * 128:(tb + 1) * 128], func=AF.Square),
                        reads=hkeys, writes=[("sq", k % 3)])

                    def mm(c4=c4, sbuf=sbuf, k=k):
                        for ci in range(4):
                            P.add("pe", lambda e, ci=ci: e.matmul(
                                psum[:, SB, tb * 128:(tb + 1) * 128], ones[:], sbuf[:, ci * 128:(ci + 1) * 128],
                                start=(c4 == 0 and ci == 0), stop=(c4 == 3 and ci == 3)),
                                reads=[("sq", k % 3), "ones"], writes=[("ps", SB)])
                    pend.append([0, mm, k % 3])
                if tb == 3:
                    stat_n[0] = KC
                    g0 = GC["ffn1"]
                    for c in range(KC):
                        P.add("dve", lambda e, c=c: e.tensor_scalar(
                            out=xnT[:, c, :], in0=hT[:, c, :], scalar1=cst[:, g0 + c:g0 + c + 1], scalar2=None,
                            op0=ALU.mult),
                            reads=hk(c) + ["cst"], writes=[("xn", c)])

            def mixer(ti):
                tt = ti % TPS
                cur, prv = tt % 2, (tt + 1) % 2
                def q_blocks():
                    proj_fm("w_in", 4, lambda kc: xnT[:, kc, :], lambda kc: [("xn", kc)], KC,
                            lambda c, b: P.add("dve", lambda e: e.tensor_tensor(
                                out=QT[:, c, :], in0=psum[:, b, :], in1=rstd[:], op=ALU.mult),
                                reads=[("ps", b), "rstd"], writes=pg(c, 1)))

                def k_blocks():
                    for bi in range(4, 8):
                        sl, wv = ws.get("w_in", bi)
                        for c2 in range(2):
                            c = (bi - 4) * 2 + c2
                            b = nb()
                            for kc in range(KC):
                                P.add("pe", lambda e, b=b, wv=wv, kc=kc, c2=c2: e.matmul(
                                    psum[:, b, :], wv[:, kc, c2 * 128:(c2 + 1) * 128], xnT[:, kc, :],
                                    start=(kc == 0), stop=(kc == KC - 1)),
                                    reads=[("ws", sl), ("xn", kc)], writes=[("ps", b)])
                            tick()
                            P.add("dve", lambda e, b=b, c=c: e.tensor_tensor(
                                out=KT[:, c, cur, :], in0=psum[:, b, :], in1=rstd[:], op=ALU.mult),
                                reads=[("ps", b), "rstd"], writes=[("KT", c, cur)])

                def v_blocks(lo, hi):
                    for bi in range(lo, hi):
                        sl, wv = ws.get("w_in", bi)
                        vb = bi - 8
                        for tb in range(4):
                            b = nb()
                            for kc in range(KC):
                                P.add("pe", lambda e, b=b, wv=wv, kc=kc, tb=tb: e.matmul(
                                    psum[:, b, 0:256], xnT[:, kc, tb * 128:(tb + 1) * 128], wv[:, kc, :],
                                    start=(kc == 0), stop=(kc == KC - 1)),
                                    reads=[("ws", sl), ("xn", kc)], writes=[("ps", b)])
                            tick()
                            evac_copy(V[:, cur * 4 + tb, vb * 4:(vb + 1) * 4, 0:64],
                                      psum[:, b, 0:256].rearrange("p (h d) -> p h d", d=64),
                                      [("ps", b), "rT"], [("V", cur * 4 + tb)], scale=rT[:, tb:tb + 1])

                pwv = wpool_sb
                UK = pg(24, 5)

                def dbuf(g):
                    return (dG, pg(13, 2)) if g % 2 == 0 else (dG1, [("dg1",)])

                def u_block(g):
                    sl, wv = ws.get("w_in", 12 + g)
                    dg, DK = dbuf(g)
                    ub = [nb(), nb()]
                    if g == 0:
                        for kc in range(KC):
                            for c2 in range(2):
                                P.add("pe", lambda e, b=ub[c2], wv=wv, kc=kc, c2=c2: e.matmul(
                                    psum[:, b, :], wv[:, kc, c2 * 128:(c2 + 1) * 128], xnT[:, kc, :],
                                    start=(kc == 0), stop=(kc == KC - 1)),
                                    reads=[("ws", sl), ("xn", kc)], writes=[("ps", ub[c2])])
                            if kc == 7:
                                norm_stats("mix")
                    for c2 in range(2):
                        b = ub[c2]
                        if g != 0:
                            for kc in range(KC):
                                P.add("pe", lambda e, b=b, wv=wv, kc=kc, c2=c2: e.matmul(
                                    psum[:, b, :], wv[:, kc, c2 * 128:(c2 + 1) * 128], xnT[:, kc, :],
                                    start=(kc == 0), stop=(kc == KC - 1)),
                                    reads=[("ws", sl), ("xn", kc)], writes=[("ps", b)])
                        tick()
                        P.add("dve", lambda e, b=b, c2=c2: e.tensor_tensor(
                            out=uG[:, c2, 16:528], in0=psum[:, b, :], in1=rstd[:], op=ALU.mult),
                            reads=[("ps", b), "rstd"], writes=UK)
                    P.add("dve", lambda e, g=g: e.tensor_copy(uG[:, :, 0:16], halo[:, 2 * g:2 * g + 2, :]),
                          reads=["halo"], writes=UK)
                    P.add("dve", lambda e, g=g: e.tensor_copy(halo[:, 2 * g:2 * g + 2, :], uG[:, :, 512:528]),
                          reads=UK, writes=["halo"])
                    src, skeys = uG, UK
                    bufs = [(tA, pg(29, 5)), (tB, pg(8, 5))]
                    sh = 1
                    for lvl in range(g + 1):
                        dst, dkeys = bufs[lvl % 2]
                        lo = 2 * sh - 1
                        P.add("dve", lambda e, dst=dst, src=src, lo=lo, sh=sh: e.tensor_tensor(
                            out=dst[:, :, lo:528], in0=src[:, :, lo:528], in1=src[:, :, lo - sh:528 - sh], op=ALU.add),
                            reads=skeys, writes=dkeys)
                        src, skeys = dst, dkeys
                        sh *= 2
                    w = 2 ** (g + 1)
                    P.add("dve", lambda e, src=src, w=w, dg=dg: e.scalar_tensor_tensor(
                        out=dg[:, :, :], in0=src[:, :, 16:528], scalar=1.0 / w, in1=uG[:, :, 16:528],
                        op0=ALU.mult, op1=ALU.subtract),
                        reads=skeys + UK, writes=DK)
                    if tt == 0:
                        for c2 in range(2):
                            P.add("dve", lambda e, src=src, w=w, c2=c2: e.tensor_tensor(
                                out=src[:, c2, 16:16 + w - 1], in0=src[:, c2, 16:16 + w - 1], in1=invc[:, 0:w - 1],
                                op=ALU.mult), reads=skeys + ["invc"], writes=skeys)
                            P.add("dve", lambda e, src=src, w=w, c2=c2, dg=dg: e.tensor_tensor(
                                out=dg[:, c2, 0:w - 1], in0=src[:, c2, 16:16 + w - 1], in1=uG[:, c2, 16:16 + w - 1],
                                op=ALU.subtract), reads=skeys + UK, writes=DK)

                def pool_mm(g):
                    dg, DK = dbuf(g)
                    for oc in range(2):
                        b = nb()
                        for k2 in range(2):
                            P.add("pe", lambda e, b=b, k2=k2, oc=oc, g=g, dg=dg: e.matmul(
                                psum[:, b, :], pwv[:, g * 2 + k2, oc * 128:(oc + 1) * 128], dg[:, k2, :],
                                start=(k2 == 0), stop=(k2 == 1)),
                                reads=["wpool"] + DK, writes=[("ps", b)])
                        tick()
                        yc = 8 + 2 * g + oc
                        evac_copy(yT[:, yc, :], psum[:, b, :], [("ps", b)], pg(8 + yc, 1),
                                  scale=cst[:, PSC + 2 * g + oc:PSC + 2 * g + oc + 1])

                u_block(0)
                q_blocks()
                u_block(1)
                pool_mm(0)
                k_blocks()
                u_block(2)
                pool_mm(1)
                make_rT()
                v_blocks(8, 10)
                u_block(3)
                pool_mm(2)
                v_blocks(10, 12)
                pool_mm(3)
                PTC = pg(24, 5)
                YK = pg(30, 4)
                iters = [(qb, h) for qb in range(4) for h in range(16)]
                NIT = len(iters)

                def kslice(qb, j):
                    kb = qb - 4 + j
                    if kb >= 0:
                        return cur, kb
                    return prv, kb + 4

                def jmin_of(qb):
                    return max(0, 4 - qb) if tt == 0 else 0

                def S(it):
                    qb, h = iters[it]
                    ch, pb = h // 2, (h % 2) * 64
                    bm, bl = 2 * (it % 2), 2 * (it % 2) + 1
                    for j in range(jmin_of(qb), 5):
                        hf, kb = kslice(qb, j)
                        rd = []
                        if j < 4:
                            o_ap, wr = psum[:, bm, j * 128:(j + 1) * 128], [("ps", bm)]
                        else:
                            o_ap, wr = psum[:, bl, 0:128], [("ps", bl)]
                        P.add("pe", lambda e, o_ap=o_ap, hf=hf, kb=kb, ch=ch, pb=pb, qb=qb: e.matmul(
                            o_ap, KT[pb:pb + 64, ch, hf, kb * 128:(kb + 1) * 128],
                            QT[pb:pb + 64, ch, qb * 128:(qb + 1) * 128], start=True, stop=True),
                            reads=[("KT", ch, hf)] + pg(ch, 1) + rd, writes=wr)

                def X(it):
                    qb, h = iters[it]
                    bm, bl, i = 2 * (it % 2), 2 * (it % 2) + 1, it % 4
                    pt = PT[i]
                    jm = jmin_of(qb)
                    kf, k3, k4 = ("pt", i, "f"), ("pt", i, "3"), ("pt", i, "4")
                    if jm < 4:
                        P.add("act", lambda e, pt=pt, bm=bm, jm=jm: e.activation(
                            out=pt[:, jm * 128:512], in_=psum[:, bm, jm * 128:512], func=AF.Exp, scale=0.125),
                            reads=[("ps", bm)] + PTC, writes=([kf] if jm < 3 else []) + [k3])
                    P.add("act", lambda e, pt=pt, bl=bl: e.activation(
                        out=pt[:, 512:640], in_=psum[:, bl, 0:128], func=AF.Exp, scale=0.125),
                        reads=[("ps", bl)] + PTC, writes=[k4])
                    if jm == 0:
                        P.add("dve", lambda e, pt=pt: e.memset(pt[0:64, 64:128], 0.0), reads=PTC, writes=[kf])
                    lo = max(jm, 3)
                    P.add("dve", lambda e, pt=pt, h=h, lo=lo: e.tensor_tensor(
                        out=pt[:, lo * 128:640], in0=pt[:, lo * 128:640], in1=E[:, h, (lo - 3) * 128:256], op=ALU.mult),
                        reads=[("E", h // 4)] + PTC, writes=([k3] if lo == 3 else []) + [k4])

                def PV(it):
                    qb, h = iters[it]
                    i = it % 4
                    pt = PT[i]
                    jm = jmin_of(qb)
                    bo, oo = 4 + h // 7, (h % 7) * 65
                    for j in range(jm, 5):
                        hf, kb = kslice(qb, j)
                        kk = ("pt", i, "f") if j < 3 else ("pt", i, str(j))
                        P.add("pe", lambda e, bo=bo, oo=oo, pt=pt, j=j, hf=hf, kb=kb, h=h, jm=jm: e.matmul(
                            psum[:, bo, oo:oo + 65], pt[:, j * 128:(j + 1) * 128], V[:, hf * 4 + kb, h, :],
                            start=(j == jm), stop=(j == 4)),
                            reads=[kk, ("V", hf * 4 + kb)] + PTC, writes=[("ps", bo)])

                def NORM(bo, h0, nh, part=None):
                    ov = psum[:, bo, 0:nh * 65].rearrange("p (h d) -> p h d", d=65)
                    if part in (None, 0):
                        P.add("dve", lambda e, ov=ov, h0=h0, nh=nh: e.reciprocal(
                            rcp[:, h0:h0 + nh].unsqueeze(2), ov[:, :, 64:65]),
                            reads=[("ps", bo)], writes=[("rcp", bo)])
                    half = (nh + 1) // 2
                    hrange = range(nh) if part is None else (range(half) if part == 0 else range(half, nh))
                    for hh in hrange:
                        h = h0 + hh
                        P.add("dve", lambda e, ov=ov, hh=hh, h=h: e.tensor_scalar(
                            out=ytok[:, h * 64:(h + 1) * 64], in0=ov[:, hh, 0:64], scalar1=rcp[:, h:h + 1],
                            scalar2=None, op0=ALU.mult),
                            reads=[("ps", bo), ("rcp", bo)] + YK, writes=[("yt", h)])

                def TRp(qb, c4):
                    for ci in range(4):
                        c = c4 * 4 + ci
                        P.add("pe", lambda e, ci=ci, c=c: e.transpose(
                            psum[:, 7, ci * 128:(ci + 1) * 128], ytok[:, c * 128:(c + 1) * 128], ident[:]),
                            reads=YK + ["ident", ("yt", 2 * c), ("yt", 2 * c + 1)], writes=[("ps", 7)])

                def TRe(qb, c4):
                    evac_copy(yT[:, c4 * 4:(c4 + 1) * 4, qb * 128:(qb + 1) * 128],
                              psum[:, 7, :].rearrange("p (c t) -> p c t", t=128),
                              [("ps", 7)], pg(8 + c4 * 4, 4))

                for it in range(min(2, NIT)):
                    S(it)
                late, late2 = [], []
                for it in range(NIT):
                    qb, h = iters[it]
                    X(it)
                    for fn in late:
                        fn()
                    late[:] = late2
                    late2 = []
                    PV(it)
                    if it + 2 < NIT:
                        S(it + 2)
                    if h == 6:
                        late.append(lambda: NORM(4, 0, 7, 0))
                        late2.append(lambda: NORM(4, 0, 7, 1))
                    elif h == 13:
                        late.append(lambda: NORM(5, 7, 7, 0))
                        late2.append(lambda: NORM(5, 7, 7, 1))
                    elif h == 15:
                        late.append(lambda: NORM(6, 14, 2))
                    if qb > 0 and h in (2, 4):
                        c4 = 0 if h == 2 else 1
                        TRp(qb - 1, c4)
                        late.append(lambda qb=qb, c4=c4: TRe(qb - 1, c4))
                for fn in late + late2:
                    fn()
                late[:] = []
                TRp(3, 0)
                TRe(3, 0)
                TRp(3, 1)
                TRe(3, 1)
                proj_fm("w_out", 8, lambda kc: yT[:, kc, :], lambda kc: pg(8 + kc, 1), KC,
                        lambda dc, b: resid_add(dc, b, xn_for="cross"))

            def cross(ti):
                qblk = [ws.get("w_cq", 0), ws.get("w_cq", 1)]
                qbank = [nb() for _ in range(4)]
                for kc in range(KC):
                    for c in range(4):
                        sl, wv = qblk[c // 2]
                        P.add("pe", lambda e, b=qbank[c], wv=wv, kc=kc, c2=c % 2: e.matmul(
                            psum[:, b, :], wv[:, kc, c2 * 128:(c2 + 1) * 128], xnT[:, kc, :],
                            start=(kc == 0), stop=(kc == KC - 1)),
                            reads=[("ws", sl), ("xn", kc)], writes=[("ps", qbank[c])])
                    if kc == 3:
                        norm_stats("cross")
                tick()
                for c in range(4):
                    P.add("dve", lambda e, c=c: e.tensor_tensor(
                        out=QcT[:, c, :], in0=psum[:, qbank[c], :], in1=rstd[:], op=ALU.mult),
                        reads=[("ps", qbank[c]), "rstd"], writes=pg(c, 1))
                sc = float(128 ** -0.5)

                def CS(h):
                    i = h % 2
                    pc = PTc[i]
                    for mb in range(2):
                        b = nb()
                        P.add("pe", lambda e, b=b, h=h, mb=mb: e.matmul(
                            psum[:, b, :], KcT[:, h, mb * 128:(mb + 1) * 128], QcT[:, h, :], start=True, stop=True),
                            reads=["KcT"] + pg(h, 1), writes=[("ps", b)])
                        P.add("act", lambda e, b=b, pc=pc, mb=mb: e.activation(
                            out=pc[:, mb, :], in_=psum[:, b, :], func=AF.Exp, scale=sc),
                            reads=[("ps", b)], writes=pg(4 + 2 * i + mb, 1))

                def CV(h):
                    i = h % 2
                    pc, pck = PTc[i], pg(4 + 2 * i, 2)
                    bd, bv = nb(), nb()
                    for mb in range(2):
                        P.add("pe", lambda e, bd=bd, pc=pc, mb=mb: e.matmul(
                            psum[:, bd, :], ones[:], pc[:, mb, :], start=(mb == 0), stop=(mb == 1)),
                            reads=pck + ["ones"], writes=[("ps", bd)])
                    for mb in range(2):
                        P.add("pe", lambda e, bv=bv, pc=pc, mb=mb, h=h: e.matmul(
                            psum[:, bv, :], Vc[:, mb, h * 128:(h + 1) * 128], pc[:, mb, :],
                            start=(mb == 0), stop=(mb == 1)),
                            reads=pck + ["Vc"], writes=[("ps", bv)])
                    P.add("dve", lambda e, bd=bd: e.reciprocal(rden, psum[:, bd, :]),
                          reads=[("ps", bd)], writes=pg(8, 2))
                    P.add("dve", lambda e, bv=bv, h=h: e.tensor_tensor(
                        out=ocT[:, h, :], in0=psum[:, bv, :], in1=rden, op=ALU.mult),
                        reads=[("ps", bv)] + pg(8, 2), writes=pg(10 + h, 1))

                CS(0)
                CS(1)
                for h in range(4):
                    CV(h)
                    if h + 2 < 4:
                        CS(h + 2)
                for bi in range(2):
                    sl, wv = ws.get("w_co", bi)
                    c0 = 0
                    if bi == 0:
                        cb = [nb() for _ in range(4)]
                        for kc in range(4):
                            for c in range(4):
                                P.add("pe", lambda e, b=cb[c], wv=wv, kc=kc, c=c: e.matmul(
                                    psum[:, b, :], wv[:, kc, c * 128:(c + 1) * 128], ocT[:, kc, :],
                                    start=(kc == 0), stop=(kc == 3)),
                                    reads=[("ws", sl)] + pg(10 + kc, 1), writes=[("ps", cb[c])])
                        tick()
                        for c in range(4):
                            resid_add(c, cb[c], xn_for="ffn2")
                        c0 = 4
                    for c2 in range(c0, 8):
                        b = nb()
                        for kc in range(4):
                            P.add("pe", lambda e, b=b, wv=wv, kc=kc, c2=c2: e.matmul(
                                psum[:, b, :], wv[:, kc, c2 * 128:(c2 + 1) * 128], ocT[:, kc, :],
                                start=(kc == 0), stop=(kc == 3)),
                                reads=[("ws", sl)] + pg(10 + kc, 1), writes=[("ps", b)])
                        tick()
                        resid_add(bi * 8 + c2, b, xn_for="ffn2")

            def final(ti, nxt):
                for tb in range(4):
                    ost = ar_f32(8 * tb, 8)
                    obank = [nb() for _ in range(4)]
                    for c4 in range(4):
                        for ci in range(4):
                            c = c4 * 4 + ci
                            P.add("pe", lambda e, b=obank[c4], ci=ci, c=c, tb=tb: e.transpose(
                                psum[:, b, ci * 128:(ci + 1) * 128], hT[:, c, tb * 128:(tb + 1) * 128], ident[:]),
                                reads=[("hT", c, tb), "ident"], writes=[("ps", obank[c4])])
                        tick()
                    early_x = (tb == 0 and nxt)
                    if tb == 0:
                        norm_stats("final")
                        hold.update(obank)
                        if early_x:
                            xpose_x(ti + 1, tb)
                        make_rT()
                        hold.clear()
                    for c4 in range(4):
                        evac_copy(ost[:, c4 * 512:(c4 + 1) * 512], psum[:, obank[c4], :],
                                  [("ps", obank[c4]), "rT"], pg(8 * tb + 2 * c4, 2), scale=rT[:, tb:tb + 1])
                    r0 = ti * T + tb * 128
                    P.add("pool", lambda e, ost=ost, r0=r0: e.dma_start(out=out_d[r0:r0 + 128, :], in_=ost),
                          reads=pg(8 * tb, 8), lane="o%d" % (tb % 2))
                    if nxt:
                        if not early_x:
                            xpose_x(ti + 1, tb)
                        if tb + 2 < 4:
                            issue_x(ti + 1, tb + 2)

            issue_x(0, 0)
            issue_x(0, 1)
            xpose_x(0, 0)
            issue_x(0, 2)
            xpose_x(0, 1)
            issue_x(0, 3)
            seq_loads(0)
            ws.emit_casts()
            P.add("sp", lambda e: e.dma_start(out=wpool_sb[:].rearrange("p k n -> p (k n)"), in_=scr["w_pool"][0]),
                  reads=[("scr", "w_pool", 0)], writes=["wpool"], lane="wp")
            xpose_x(0, 2)
            xpose_x(0, 3)
            for ti in range(ntiles):
                if ti % TPS == 0:
                    if ti > 0:
                        seq_loads(ti // TPS)
                    seq_start(ti // TPS)
                ffn("ffn1", "mix")
                mixer(ti)
                cross(ti)
                nxt = ti + 1 < ntiles
                if nxt:
                    issue_x(ti + 1, 0)
                    issue_x(ti + 1, 1)
                ffn("ffn2", "final")
                final(ti, nxt)

        class WS:
            def __init__(self, P, plan=None):
                self.P = P
                self.plan = plan
                self.rec = []
                self.i = 0
                self.loaded = 0
                self.seen = set()

            def view(self, name, slot):
                _, kc, _, nbw = wblocks[name][0]
                return wsl[slot][:, 0:kc * nbw].rearrange("p (k n) -> p k n", n=nbw)

            def emit_casts(self):
                if self.plan is None:
                    return
                todo = [("w_pool", 0)]
                if not DIRECT:
                    for nb_ in self.plan:
                        if nb_ not in todo:
                            todo.append(nb_)
                for n, (name, bi) in enumerate(todo):
                    self.seen.add((name, bi))
                    r0, kc, c0, nbw = wblocks[name][bi]
                    src = W[name][r0:r0 + kc * 128, c0:c0 + nbw].rearrange("(k p) n -> p k n", p=128)
                    dst = scr[name][bi].rearrange("p (k n) -> p k n", n=nbw)
                    self.P.add("pool", lambda e, src=src, dst=dst: e.dma_start(out=dst, in_=src),
                               writes=[("scr", name, bi)], lane="c%d" % (n % 8))

            def _load(self, k):
                name, bi = self.plan[k]
                slot = k % NSLOT
                r0, kc, c0, nbw = wblocks[name][bi]
                if (name, bi) not in self.seen:
                    self.seen.add((name, bi))
                    src = W[name][r0:r0 + kc * 128, c0:c0 + nbw].rearrange("(k p) n -> p k n", p=128)
                    dstv = wsl[slot][:, 0:kc * nbw].rearrange("p (k n) -> p k n", n=nbw)
                    self.P.add("pool", lambda e, src=src, dstv=dstv: e.dma_start(out=dstv, in_=src),
                               writes=[("ws", slot)], lane="v%d" % slot)
                    self.P.add("sp", lambda e, name=name, bi=bi, slot=slot, kc=kc, nbw=nbw: e.dma_start(
                        out=scr[name][bi], in_=wsl[slot][:, 0:kc * nbw]),
                        reads=[("ws", slot)], writes=[("scr", name, bi)], lane="s%d" % slot)
                else:
                    self.P.add("sp", lambda e, name=name, bi=bi, slot=slot, kc=kc, nbw=nbw: e.dma_start(
                        out=wsl[slot][:, 0:kc * nbw], in_=scr[name][bi]),
                        reads=[("scr", name, bi)], writes=[("ws", slot)], lane="w%d" % slot)

            def get(self, name, bi):
                k = self.i
                self.i += 1
                if self.plan is None:
                    self.rec.append((name, bi))
                    return 0, self.view(name, 0)
                assert self.plan[k] == (name, bi)
                while self.loaded < min(k + NSLOT - 2, len(self.plan) - 1) + 1:
                    self._load(self.loaded)
                    self.loaded += 1
                return k % NSLOT, self.view(name, k % NSLOT)

        dryP = Prog(dry=True)
        dws = WS(dryP)
        emit_all(dryP, dws)
        P = Prog()
        ws = WS(P, plan=dws.rec)
        emit_all(P, ws)
        P.emit(nc, ctx)
    return nc


def _host_tables(rel_bias, norms, pool_scale):
    cst = np.zeros((128, 128), np.float32)
    for i, g in enumerate(norms):
        cst[:, 16 * i:16 * i + 16] = np.asarray(g, np.float32).reshape(16, 128).T
    cst[:, 96:104] = np.asarray(pool_scale, np.float32).reshape(8, 128).T
    rb = np.asarray(rel_bias, np.float32).reshape(16, 257)
    cst[:, 104:120] = rb[:, 256][None, :]
    cst[:, 127] = EPS
    k = np.arange(128)[:, None]
    q = np.arange(128)[None, :]
    idx3 = np.clip(128 + q - k, -128, 128) + 128
    idx4 = np.clip(q - k, -128, 128) + 128
    bm = np.stack([rb[:, idx3], rb[:, idx4]], axis=2)
    bm = np.ascontiguousarray(bm.transpose(1, 0, 2, 3)).reshape(128, 16 * 256)
    return cst, bm


_NC_CACHE = {}


def kernel(x, mem, ffn1_norm, ffn1_w_gate, ffn1_w_up, ffn1_w_down, mix_norm, w_in, rel_bias, w_pool,
           pool_scale, w_out, cross_norm, mem_norm, w_cq, w_ckv, w_co, ffn2_norm, ffn2_w_gate,
           ffn2_w_up, ffn2_w_down, final_norm):
    f = lambda a: np.ascontiguousarray(np.asarray(a, dtype=np.float32))
    x = f(x)
    mem = f(mem)
    cst, bm = _host_tables(rel_bias, [f(ffn1_norm)[0], f(mix_norm)[0], f(cross_norm)[0], f(ffn2_norm)[0],
                                      f(final_norm), f(mem_norm)[0]], f(pool_scale)[0])
    shared = {
        "cst": cst, "bmat": bm,
        "ffn1_g": f(ffn1_w_gate)[0], "ffn1_u": f(ffn1_w_up)[0], "ffn1_d": f(ffn1_w_down)[0],
        "ffn2_g": f(ffn2_w_gate)[0], "ffn2_u": f(ffn2_w_up)[0], "ffn2_d": f(ffn2_w_down)[0],
        "w_in": f(w_in)[0], "w_pool": f(w_pool)[0].reshape(1024, 256), "w_out": f(w_out)[0],
        "w_cq": f(w_cq)[0], "w_ckv": f(w_ckv)[0], "w_co": f(w_co)[0],
    }
    if "nc" not in _NC_CACHE:
        _NC_CACHE["nc"] = build_program()
    nc = _NC_CACHE["nc"]
    in_maps = []
    for c in range(NCORES):
        m = dict(shared)
        m["x"] = x[2 * c:2 * c + 2].reshape(NSEQ * SEQ, D)
        m["mem"] = mem[2 * c:2 * c + 2].reshape(NSEQ * NMEM, D)
        in_maps.append(m)
    res = run_bass_kernel_spmd(nc, in_maps, core_ids=list(range(NCORES)))
    out = np.stack([np.asarray(r["out"], dtype=np.float32).reshape(NSEQ, SEQ, D) for r in res.results], axis=0)
    return out.reshape(16, SEQ, D)
```
